# Optimizing a Trainium2 kernel written in Bass

```python
import jax, jax.numpy as jnp
from jax import lax
import numpy as np

D_MODEL = 1024
BATCH = 16
SEQ = 2048
DEPTH = 2

N_A_LAYERS = DEPTH // 2
N_B_LAYERS = DEPTH - N_A_LAYERS
LRU_WIDTH = 1344
LRU_BLOCKS = 8
LRU_BLOCK_DIM = LRU_WIDTH // LRU_BLOCKS
LRU_CONV_WIDTH = 4
RG_C = 8.0
N_HEADS = 16
N_KV_HEADS = 2
HEAD_DIM = 64
GROUP = N_HEADS // N_KV_HEADS
ROPE_DIM = HEAD_DIM // 4
ROPE_THETA = 500000.0
WINDOW = 128
BLOCK = 128
D_FF = 3 * D_MODEL
FFN_CONV_WIDTH = 3
EPS = 1e-6
MAX_POS_OFFSET = 4096
MASK_VALUE = -1e30

kernel_name = "yoco_hawk_swa_sink_convffn_trunk"


def rms_norm(x, g):
    xf = x.astype(jnp.float32)
    var = jnp.mean(xf * xf, axis=-1, keepdims=True)
    return (xf * lax.rsqrt(var + EPS) * g.astype(jnp.float32)).astype(x.dtype)


def causal_depthwise_conv(x, w, b):
    k_w = w.shape[0]
    s = x.shape[1]
    xp = jnp.pad(x, ((0, 0), (k_w - 1, 0), (0, 0)))
    out = b
    for k in range(k_w):
        out = out + xp[:, k:k + s] * w[k]
    return out


def rope_partial(x, positions):
    half = ROPE_DIM // 2
    inv_freq = ROPE_THETA ** (-jnp.arange(0, ROPE_DIM, 2, dtype=jnp.float32) / ROPE_DIM)
    ang = positions.astype(jnp.float32)[..., None] * inv_freq
    cos = jnp.cos(ang)[:, :, None, :]
    sin = jnp.sin(ang)[:, :, None, :]
    xf = x.astype(jnp.float32)
    x1 = xf[..., :half]
    x2 = xf[..., half:ROPE_DIM]
    rot = jnp.concatenate([x1 * cos - x2 * sin, x2 * cos + x1 * sin, xf[..., ROPE_DIM:]], axis=-1)
    return rot.astype(x.dtype)


def _lin_combine(e1, e2):
    a1, b1 = e1
    a2, b2 = e2
    return a1 * a2, a2 * b1 + b2


def rg_lru(xr, w_rg, b_rg, w_ig, b_ig, lam):
    bsz, s, w = xr.shape
    xb = xr.reshape(bsz, s, LRU_BLOCKS, LRU_BLOCK_DIM)
    r = jax.nn.sigmoid(jnp.einsum('bshi,hij->bshj', xb, w_rg).reshape(bsz, s, w).astype(jnp.float32) + b_rg.astype(jnp.float32))
    i = jax.nn.sigmoid(jnp.einsum('bshi,hij->bshj', xb, w_ig).reshape(bsz, s, w).astype(jnp.float32) + b_ig.astype(jnp.float32))
    log_a = -RG_C * r * jax.nn.softplus(-lam.astype(jnp.float32))
    a = jnp.exp(log_a)
    u = jnp.sqrt(-jnp.expm1(2.0 * log_a)) * i * xr.astype(jnp.float32)
    _, h = lax.associative_scan(_lin_combine, (a, u), axis=1)
    return h.astype(xr.dtype)


def recurrent_block(hn, w_gate, w_in, conv_w, conv_b, w_rg, b_rg, w_ig, b_ig, lam, w_out):
    gate = jax.nn.gelu(hn @ w_gate)
    xr = causal_depthwise_conv(hn @ w_in, conv_w, conv_b)
    y = rg_lru(xr, w_rg, b_rg, w_ig, b_ig, lam)
    return (gate * y) @ w_out


def conv_ffn(hn, w_up, conv_w, conv_b, w_down):
    up = causal_depthwise_conv(hn @ w_up, conv_w, conv_b)
    gate, val = jnp.split(up, 2, axis=-1)
    return (jax.nn.gelu(gate) * val) @ w_down


def shared_kv(h, kv_norm, w_k, w_v, k_norm, positions):
    bsz, s, _ = h.shape
    hn = rms_norm(h, kv_norm)
    k = (hn @ w_k).reshape(bsz, s, N_KV_HEADS, HEAD_DIM)
    k = rope_partial(rms_norm(k, k_norm), positions)
    v = (hn @ w_v).reshape(bsz, s, N_KV_HEADS, HEAD_DIM)
    return k, v


def banded_sink_attention(q, k, v, sinks):
    bsz, s = q.shape[0], q.shape[1]
    nb = s // BLOCK
    qb = q.reshape(bsz, nb, BLOCK, N_KV_HEADS, GROUP, HEAD_DIM)

    def band(t):
        tb = t.reshape(bsz, nb, BLOCK, N_KV_HEADS, HEAD_DIM)
        prev = jnp.concatenate([jnp.zeros_like(tb[:, :1]), tb[:, :-1]], axis=1)
        return jnp.concatenate([prev, tb], axis=2)

    kband = band(k)
    vband = band(v)
    scale = HEAD_DIM ** -0.5
    scores = jnp.einsum('bnqhgd,bnkhd->bnhgqk', qb, kband).astype(jnp.float32) * scale
    qpos = jnp.arange(BLOCK)
    kpos = jnp.arange(2 * BLOCK) - BLOCK
    diff = qpos[:, None] - kpos[None, :]
    in_band = (diff >= 0) & (diff < WINDOW)
    valid_key = (jnp.arange(nb)[:, None] * BLOCK + kpos[None, :]) >= 0
    mask = in_band[None, :, :] & valid_key[:, None, :]
    scores = jnp.where(mask[None, :, None, None], scores, MASK_VALUE)
    sink = sinks.astype(jnp.float32).reshape(1, 1, N_KV_HEADS, GROUP, 1, 1)
    m = jnp.maximum(jnp.max(scores, axis=-1, keepdims=True), sink)
    p = jnp.exp(scores - m)
    denom = jnp.sum(p, axis=-1, keepdims=True) + jnp.exp(sink - m)
    probs = (p / denom).astype(v.dtype)
    o = jnp.einsum('bnhgqk,bnkhd->bnqhgd', probs, vband)
    return o.reshape(bsz, s, N_HEADS * HEAD_DIM)


def swa_layer(hn, k, v, positions, w_q, q_norm, sinks, w_o):
    bsz, s, _ = hn.shape
    q = (hn @ w_q).reshape(bsz, s, N_HEADS, HEAD_DIM)
    q = rope_partial(rms_norm(q, q_norm), positions)
    return banded_sink_attention(q, k, v, sinks) @ w_o


def setup_inputs(seed: int = 0) -> dict:
    key = jax.random.key(seed)
    ks = jax.random.split(key, 32)
    f32 = jnp.float32

    def nrm(k, shape, fan_in):
        return jax.random.normal(k, shape, f32) * (fan_in ** -0.5)

    def gain(k, shape):
        return 1.0 + 0.05 * jax.random.normal(k, shape, f32)

    def bias(k, shape):
        return 0.02 * jax.random.normal(k, shape, f32)

    x = jax.random.normal(ks[0], (BATCH, SEQ, D_MODEL), f32)
    offsets = jax.random.randint(ks[1], (BATCH, 1), 0, MAX_POS_OFFSET, dtype=jnp.int32)
    positions = (offsets + jnp.arange(SEQ, dtype=jnp.int32)[None, :]).astype(jnp.int32)

    u = jax.random.uniform(ks[2], (N_A_LAYERS, LRU_WIDTH), f32, minval=0.9, maxval=0.999)
    s_a = u ** (1.0 / RG_C)
    lam = jnp.log(s_a) - jnp.log1p(-s_a)

    qkv_w = N_HEADS * HEAD_DIM
    kv_w = N_KV_HEADS * HEAD_DIM
    return {
        "x": x,
        "positions": positions,
        "a_norm": gain(ks[3], (N_A_LAYERS, D_MODEL)),
        "a_w_gate": nrm(ks[4], (N_A_LAYERS, D_MODEL, LRU_WIDTH), D_MODEL),
        "a_w_in": nrm(ks[5], (N_A_LAYERS, D_MODEL, LRU_WIDTH), D_MODEL),
        "a_conv_w": nrm(ks[6], (N_A_LAYERS, LRU_CONV_WIDTH, LRU_WIDTH), LRU_CONV_WIDTH),
        "a_conv_b": bias(ks[7], (N_A_LAYERS, LRU_WIDTH)),
        "a_w_rg": nrm(ks[8], (N_A_LAYERS, LRU_BLOCKS, LRU_BLOCK_DIM, LRU_BLOCK_DIM), LRU_BLOCK_DIM),
        "a_b_rg": bias(ks[9], (N_A_LAYERS, LRU_WIDTH)),
        "a_w_ig": nrm(ks[10], (N_A_LAYERS, LRU_BLOCKS, LRU_BLOCK_DIM, LRU_BLOCK_DIM), LRU_BLOCK_DIM),
        "a_b_ig": bias(ks[11], (N_A_LAYERS, LRU_WIDTH)),
        "a_lam": lam,
        "a_w_out": nrm(ks[12], (N_A_LAYERS, LRU_WIDTH, D_MODEL), LRU_WIDTH),
        "kv_norm": gain(ks[13], (D_MODEL,)),
        "w_k": nrm(ks[14], (D_MODEL, kv_w), D_MODEL),
        "w_v": nrm(ks[15], (D_MODEL, kv_w), D_MODEL),
        "k_norm": gain(ks[16], (HEAD_DIM,)),
        "b_norm": gain(ks[17], (N_B_LAYERS, D_MODEL)),
        "b_w_q": nrm(ks[18], (N_B_LAYERS, D_MODEL, qkv_w), D_MODEL),
        "q_norm": gain(ks[19], (N_B_LAYERS, HEAD_DIM)),
        "sinks": 0.5 * jax.random.normal(ks[20], (N_B_LAYERS, N_HEADS), f32),
        "b_w_o": nrm(ks[21], (N_B_LAYERS, qkv_w, D_MODEL), qkv_w),
        "f_norm": gain(ks[22], (DEPTH, D_MODEL)),
        "f_w_up": nrm(ks[23], (DEPTH, D_MODEL, 2 * D_FF), D_MODEL),
        "f_conv_w": nrm(ks[24], (DEPTH, FFN_CONV_WIDTH, 2 * D_FF), FFN_CONV_WIDTH),
        "f_conv_b": bias(ks[25], (DEPTH, 2 * D_FF)),
        "f_w_down": nrm(ks[26], (DEPTH, D_FF, D_MODEL), D_FF),
    }


def reference(x, positions, a_norm, a_w_gate, a_w_in, a_conv_w, a_conv_b, a_w_rg, a_b_rg,
              a_w_ig, a_b_ig, a_lam, a_w_out, kv_norm, w_k, w_v, k_norm, b_norm, b_w_q,
              q_norm, sinks, b_w_o, f_norm, f_w_up, f_conv_w, f_conv_b, f_w_down):
    h = x
    k_sh = None
    v_sh = None
    for layer in range(DEPTH):
        if layer < N_A_LAYERS:
            i = layer
            h = h + recurrent_block(rms_norm(h, a_norm[i]), a_w_gate[i], a_w_in[i], a_conv_w[i],
                                    a_conv_b[i], a_w_rg[i], a_b_rg[i], a_w_ig[i], a_b_ig[i],
                                    a_lam[i], a_w_out[i])
        else:
            j = layer - N_A_LAYERS
            h = h + swa_layer(rms_norm(h, b_norm[j]), k_sh, v_sh, positions, b_w_q[j],
                              q_norm[j], sinks[j], b_w_o[j])
        h = h + conv_ffn(rms_norm(h, f_norm[layer]), f_w_up[layer], f_conv_w[layer],
                         f_conv_b[layer], f_w_down[layer])
        if layer == N_A_LAYERS - 1:
            k_sh, v_sh = shared_kv(h, kv_norm, w_k, w_v, k_norm, positions)
    return h
```

```python
import numpy as np
import concourse.bass as bass
import concourse.mybir as mybir
from concourse.bass_utils import run_bass_kernel_spmd

F32 = mybir.dt.float32
BF16 = mybir.dt.bfloat16
I32 = mybir.dt.int32
AF = mybir.ActivationFunctionType
ALU = mybir.AluOpType

NCORES = 8
D = 1024
S = 2048
NSEQ = 2
T = 512
NT = S // T
KD = D // 128
LW = 1344
LWP = 1408
NJ = LWP // 128
LBLK = 168
FF = 3072
NPAIR = FF // 128
NH = 16
HD = 64
EPS = 1e-6
ROPE_DIM = 16
ROPE_THETA = 500000.0

ENGS = ("pe", "act", "dve", "pool", "sp")


class Buf:
    __slots__ = ("name", "space", "lo", "hi", "writers", "readers", "overl")

    def __init__(self, name, space, lo, hi):
        self.name, self.space, self.lo, self.hi = name, space, lo, hi
        self.writers = []
        self.readers = []
        self.overl = []


class Op:
    __slots__ = ("eng", "fn", "deps", "signal", "sem", "val", "is_dma", "dsem", "idx")

    def __init__(self, eng, fn, is_dma=False, dsem=None):
        self.eng, self.fn, self.is_dma, self.dsem = eng, fn, is_dma, dsem
        self.deps = []
        self.signal = False
        self.sem = None
        self.val = None


class Prog:
    def __init__(self, nc):
        self.nc = nc
        self.q = {e: [] for e in ENGS}
        self.bufs = {"sb": [], "ps": [], "dram": []}
        self.dma_counts = {}

    def buf(self, name, space, lo, hi):
        b = Buf(name, space, lo, hi)
        for o in self.bufs[space]:
            if o.lo < hi and lo < o.hi:
                o.overl.append(b)
                b.overl.append(o)
        self.bufs[space].append(b)
        return b

    def op(self, eng, fn, reads=(), writes=(), is_dma=False, dsem=None, extra_deps=()):
        o = Op(eng, fn, is_dma, dsem)
        deps = []
        for b in reads:
            deps.extend(b.writers)
            for ob in b.overl:
                deps.extend(ob.writers)
            if b.space == "ps":
                deps.extend(r for r in b.readers if r.eng != eng)
        for b in writes:
            deps.extend(b.writers)
            deps.extend(b.readers)
            for ob in b.overl:
                deps.extend(ob.writers)
                deps.extend(ob.readers)
        deps.extend(extra_deps)
        seen = set()
        for d in deps:
            if d is o or id(d) in seen:
                continue
            seen.add(id(d))
            if (not d.is_dma) and (not is_dma) and d.eng == "pe" and eng == "pe":
                continue
            o.deps.append(d)
            d.signal = True
        for b in writes:
            b.writers = [o]
            b.readers = []
            for ob in b.overl:
                ob.writers = [w for w in ob.writers]
                ob.readers = []
                ob.writers = [o]
        for b in reads:
            b.readers.append(o)
        if is_dma:
            o.signal = True
        self.q[eng].append(o)
        return o


class Arena:
    def __init__(self, P, ap, nwords):
        self.P, self.ap, self.n = P, ap, nwords
        self.top = 0
        self.cnt = 0

    def mark(self):
        return self.top

    def reset(self, m):
        self.top = m

    def alloc(self, name, shape, dtype, track=True):
        n = int(np.prod(shape))
        words = n if dtype in (F32, I32) else (n + 1) // 2
        words = (words + 7) // 8 * 8
        lo = self.top
        lim = getattr(self, "limit", None) or self.n
        assert lo + words <= lim or getattr(self, "in_kv", False), f"arena overflow allocating {name}: {lo}+{words} > {lim}"
        self.top += words
        v = self.ap[:, lo:lo + words]
        if dtype != F32:
            v = v.bitcast(dtype)
        v = v[:, 0:n]
        if len(shape) == 2:
            v = v.rearrange("p (a b) -> p a b", a=shape[0])
        elif len(shape) == 3:
            v = v.rearrange("p (a b c) -> p a b c", a=shape[0], b=shape[1])
        elif len(shape) == 4:
            v = v.rearrange("p (a b c d) -> p a b c d", a=shape[0], b=shape[1], c=shape[2])
        self.cnt += 1
        b = self.P.buf(f"{name}#{self.cnt}", "sb", lo * 4, (lo + words) * 4) if track else None
        return v, b


def alloc_parts(A, name, nparts, part_shape, dtype):
    n = int(np.prod(part_shape))
    esz = 4 if dtype in (F32, I32) else 2
    lo = A.top
    v, _ = A.alloc(name, [nparts] + list(part_shape), dtype, track=False)
    bufs = [A.P.buf(f"{name}[{i}]#{A.cnt}", "sb", lo * 4 + i * n * esz, lo * 4 + (i + 1) * n * esz) for i in range(nparts)]
    return v, bufs


def gate_tile_list():
    tiles = []
    for j in range(NJ):
        c0, c1 = j * 128, min(j * 128 + 128, LW)
        if c0 >= LW:
            continue
        b0, b1 = c0 // LBLK, (c1 - 1) // LBLK
        r0, r1 = b0 * LBLK, (b1 + 1) * LBLK
        for kk in range(r0 // 128, (r1 - 1) // 128 + 1):
            tiles.append((j, kk))
    return tiles


GT = gate_tile_list()
NGT = len(GT)

STAGES = ("X", "R", "F0", "K", "A", "F1")
SEQ_K = False


def build_program(stop_after="F1", nseq=NSEQ):
    nc = bass.Bass("TRN2", target_bir_lowering=False)
    P = Prog(nc)
    last_stage = STAGES.index(stop_after)

    def dram_in(name, shape, dt=F32):
        return nc.dram_tensor(name, list(shape), dt, kind="ExternalInput").ap()

    def dram_scratch(name, shape, dt=BF16):
        return nc.dram_tensor(name, list(shape), dt, kind="Internal").ap()

    x_d = dram_in("x", [NSEQ, S, D])
    pos_d = dram_in("pos", [NSEQ, S], I32)
    y_d = nc.dram_tensor("y", [NSEQ, S, D], F32, kind="ExternalOutput").ap()

    wspecs = [
        ("w_in", [NJ, 128, KD * 128]),
        ("w_gate", [NJ, 128, KD * 128]),
        ("w_rg", [128, NGT * 128]),
        ("w_ig", [128, NGT * 128]),
        ("w_out", [KD, 128, NJ * 128]),
        ("w_up0", [NPAIR, 128, KD * 2 * 128]),
        ("w_down0", [128, NPAIR * D]),
        ("w_k", [128, KD * 128]),
        ("w_v", [128, KD * 128]),
        ("w_q", [128, KD * D]),
        ("w_o", [64, NH * D]),
        ("w_up1", [NPAIR, 128, KD * 2 * 128]),
        ("w_down1", [128, NPAIR * D]),
    ]
    wgroup = {"w_in": 0, "w_gate": 0, "w_rg": 0, "w_ig": 0, "w_out": 0, "w_up0": 1, "w_down0": 1,
              "w_k": 2, "w_v": 2, "w_q": 2, "w_o": 2, "w_up1": 3, "w_down1": 3}
    wf = {n: dram_in(n, s) for n, s in wspecs}
    wb = {n: dram_scratch(n + "_b", s) for n, s in wspecs}

    gains_d = dram_in("gains", [128, 5 * KD])
    recc_d = dram_in("rec_c", [128, NJ * 8])
    ffnc_d = dram_in("ffn_c", [128, 2 * 48 * 4])
    qkg_d = dram_in("qk_g", [128, 2])
    sinks_d = dram_in("sinks_row", [1, NH])
    identf_d = dram_in("ident_f", [128, 128])
    cmat_d = dram_in("cmat", [128, 3 * 128])
    invf_d = dram_in("invf", [128, 1])

    from contextlib import ExitStack
    es = ExitStack()
    ARENA_WORDS = 53200
    arena_t = es.enter_context(nc.sbuf_tensor("arena", [128, ARENA_WORDS], F32))
    psum_t = es.enter_context(nc.psum_tensor("psum", [128, 8 * 512], F32))
    A = Arena(P, arena_t[:], ARENA_WORDS)
    banks = []
    for b in range(8):
        banks.append((psum_t[:, b * 512:(b + 1) * 512], P.buf(f"bank{b}", "ps", b * 2048, (b + 1) * 2048)))

    nsem = {"pe": 3, "act": 4, "dve": 5, "pool": 2, "sp": 1}
    sems = {e: [es.enter_context(nc.semaphore(f"s_{e}{i}")) for i in range(n)] for e, n in nsem.items()}
    dma_sems = {}

    def dsem(name):
        if name not in dma_sems:
            dma_sems[name] = es.enter_context(nc.semaphore("d_" + name))
        return dma_sems[name]

    def dma(eng, out, in_, reads, writes, sem, **kw):
        return P.op(eng, lambda e: e.dma_start(out=out, in_=in_, **kw), reads=reads, writes=writes,
                    is_dma=True, dsem=dsem(sem))

    def mm(out, lhsT, rhs, start, stop, reads, writes):
        return P.op("pe", lambda e: e.matmul(out, lhsT, rhs, start=start, stop=stop), reads=reads, writes=writes)

    def act(out, in_, func, reads, writes, bias=None, scale=None, eng="act"):
        kw = {}
        if bias is not None:
            kw["bias"] = bias
        if scale is not None:
            kw["scale"] = scale
        return P.op(eng, lambda e: e.activation(out=out, in_=in_, func=func, **kw), reads=reads, writes=writes)

    def tt(out, in0, in1, op, reads, writes, eng="dve"):
        return P.op(eng, lambda e: e.tensor_tensor(out=out, in0=in0, in1=in1, op=op), reads=reads, writes=writes)

    def stt(out, in0, scalar, in1, op0, op1, reads, writes):
        return P.op("dve", lambda e: e.scalar_tensor_tensor(out=out, in0=in0, scalar=scalar, in1=in1, op0=op0, op1=op1),
                    reads=reads, writes=writes)

    def ts(out, in0, s1, s2, op0, op1, reads, writes, eng="dve"):
        return P.op(eng, lambda e: e.tensor_scalar(out=out, in0=in0, scalar1=s1, scalar2=s2, op0=op0, op1=op1),
                    reads=reads, writes=writes)

    def cp(eng, out, in_, reads, writes):
        if eng == "act":
            return P.op("act", lambda e: e.activation(out=out, in_=in_, func=AF.Copy), reads=reads, writes=writes)
        return P.op(eng, lambda e: e.tensor_copy(out=out, in_=in_), reads=reads, writes=writes)

    h, _ = A.alloc("h", [KD, S], F32, track=False)
    hb = [[P.buf(f"h{k}_{t}", "sb", (k * S + t * T) * 4, (k * S + t * T + T) * 4) for t in range(NT)]
          for k in range(KD)]
    identf, identf_b = A.alloc("identf", [128], F32)
    cmat, cmat_b = A.alloc("cmat", [3, 128], BF16)
    ones_bf, bones_bf, rt_bf = cmat[:, 0, :], cmat[:, 1, :], cmat[:, 2, :]
    gains, gains_b = A.alloc("gains", [5, KD], F32)
    recc, recc_b = A.alloc("recc", [NJ, 8], F32)
    ffnc, ffnc_b = A.alloc("ffnc", [2, 48, 4], F32)
    qkg, qkg_b = A.alloc("qkg", [2], F32)
    invf, invf_b = A.alloc("invf", [1], F32)

    dma("sp", identf, identf_d, [], [identf_b], "const1")
    dma("pool", cmat, cmat_d.rearrange("p (a b) -> p a b", a=3), [], [cmat_b], "constc")
    dma("sp", gains, gains_d.rearrange("p (a b) -> p a b", a=5), [], [gains_b], "const2")
    dma("sp", recc, recc_d.rearrange("p (a b) -> p a b", a=NJ), [], [recc_b], "const3")
    dma("sp", ffnc, ffnc_d.rearrange("p (a b c) -> p a b c", a=2, b=48), [], [ffnc_b], "const4")
    dma("sp", qkg, qkg_d, [], [qkg_b], "const5")
    dma("sp", invf, invf_d, [], [invf_b], "const6")

    wdram_b = {n: P.buf("wd_" + n, "dram", 0, 0) for n, _ in wspecs}
    wshape = dict(wspecs)

    def cast_weight(n, defer=False):
        sh = wshape[n]
        tot = int(np.prod(sh))
        b = wdram_b[n]
        src = wf[n]
        dst = wb[n]
        if len(sh) == 3:
            src = src.rearrange("a p f -> (a p f)")
            dst = dst.rearrange("a p f -> (a p f)")
        else:
            src = src.rearrange("p f -> (p f)")
            dst = dst.rearrange("p f -> (p f)")
        assert tot % 2048 == 0
        src = src.rearrange("(r c) -> r c", c=2048)
        dst = dst.rearrange("(r c) -> r c", c=2048)
        rows, cols = src.shape
        step = max(16, (1 << 20) // cols // 16 * 16)
        r = 0
        while r < rows:
            rr = min(step, rows - r)

            def issue(o=dst[r:r + rr, :], i=src[r:r + rr, :], b=b, n=n):
                P.op("pool", lambda e: e.dma_start(out=o, in_=i), writes=[b], is_dma=True, dsem=dsem(f"wc_{n}"))
            if defer:
                pending_casts.append((n, issue))
            else:
                issue()
            r += rr

    pending_casts = []

    def drip(k=1):
        for _ in range(k):
            if pending_casts:
                pending_casts.pop(0)[1]()

    def flush_casts(names):
        while any(n in names for n, _ in pending_casts):
            pending_casts.pop(0)[1]()

    for n in ("w_in", "w_gate", "w_rg", "w_ig", "w_out"):
        cast_weight(n)

    cst, cst_b = A.alloc("cst", [8], F32)
    CE, CQ, C1, CNPI = cst[:, 0:1], cst[:, 1:2], cst[:, 2:3], cst[:, 3:4]
    for col, val in ((0, EPS), (1, 0.25), (2, 1.0), (3, -float(np.pi))):
        P.op("dve", lambda e, c=col, v=val: e.memset(cst[:, c:c + 1], v), writes=[cst_b])
    rder, rder_b = A.alloc("rder", [3, NJ], F32)
    sk, sk_b = A.alloc("sk", [NH], F32)
    state, _ = A.alloc("state", [NJ], F32, track=False)
    stb = [P.buf(f"st{j}", "sb", 0, 0) for j in range(NJ)]
    tails_r, _ = A.alloc("tails_r", [2, NJ, 4], F32, track=False)
    trb = [[P.buf(f"tr{p}_{j}", "sb", 0, 0) for j in range(NJ)] for p in range(2)]
    tails_f, _ = A.alloc("tails_f", [2, 48, 2], F32, track=False)
    tfb = [[P.buf(f"tf{p}_{c}", "sb", 0, 0) for c in range(48)] for p in range(2)]

    lam = recc[:, :, 7]
    tmpc, tmpc_b = A.alloc("tmpc", [NJ], F32)
    act(tmpc, lam, AF.Exp, [recc_b], [tmpc_b], scale=-1.0)
    act(tmpc, tmpc, AF.Ln, [tmpc_b, cst_b], [tmpc_b], bias=C1)
    ts(rder[:, 0, :], tmpc, -4.0, None, ALU.mult, ALU.bypass, [tmpc_b], [rder_b])
    ts(rder[:, 1, :], recc[:, :, 5], 0.5, None, ALU.mult, ALU.bypass, [recc_b], [rder_b])
    ts(rder[:, 2, :], recc[:, :, 6], 0.5, None, ALU.mult, ALU.bypass, [recc_b], [rder_b])
    dma("sp", sk[0:1, :], sinks_d, [], [sk_b], "const_sk")
    act(sk[0:1, :], sk[0:1, :], AF.Exp, [sk_b], [sk_b])

    m0 = A.mark()
    print('persistent words', m0, 'phase budget KiB', (ARENA_WORDS - m0) * 4 / 1024)
    KV_WORDS = (S // 2) + (S // 2) + (NH * 128 // 2)
    kv_lo = ARENA_WORDS - KV_WORDS - 8
    TWO_PI = float(2 * np.pi)

    def tsl(t):
        return slice(t * T, (t + 1) * T)

    def recip(out, in_, reads, writes):
        return P.op("dve", lambda e: e.reciprocal_approx_fast(out=out, in_=in_), reads=reads, writes=writes)

    def norm_part1(gi, t, sq, sq_b, rstd, rstd_b, bankid):
        bk, bkb = banks[bankid]
        act(sq, h[:, :, tsl(t)], AF.Square, [hb[k][t] for k in range(KD)], [sq_b])
        for k in range(KD):
            mm(bk, ones_bf, sq[:, k, :], k == 0, k == KD - 1, [sq_b, cmat_b], [bkb])
        act(rstd, bk, AF.Ln, [bkb, cst_b], [rstd_b], bias=CE, scale=1.0 / D)
        act(rstd, rstd, AF.Exp, [rstd_b], [rstd_b], scale=-0.5)

    def norm_part2(gi, t, xn, xn_b, rstd, rstd_b):
        for k in range(KD):
            stt(xn[:, k, :], h[:, k, tsl(t)], gains[:, gi, k:k + 1], rstd, ALU.mult, ALU.mult,
                [hb[k][t], gains_b, rstd_b], [xn_b])

    def wload(name, unit_ap, slot, slot_b, semname):
        return dma("sp", slot, unit_ap, [wdram_b[name]], [slot_b], semname)

    def phase_R(s):
        A.reset(m0)
        A.limit = None
        wrg, wrg_b = A.alloc("wrg", [NGT, 128], BF16)
        wig, wig_b = A.alloc("wig", [NGT, 128], BF16)
        wload("w_rg", wb["w_rg"].rearrange("p (a b) -> p a b", a=NGT), wrg, wrg_b, "wrg")
        wload("w_ig", wb["w_ig"].rearrange("p (a b) -> p a b", a=NGT), wig, wig_b, "wig")
        NWS = 2
        win_s = [A.alloc(f"win{i}", [KD, 128], BF16) for i in range(NWS)]
        wgt_s = [A.alloc(f"wgt{i}", [KD, 128], BF16) for i in range(NWS)]
        wout_s = [A.alloc(f"wout{i}", [NJ, 128], BF16) for i in range(2)]
        xns = [A.alloc(f"xn{i}", [KD, T], BF16) for i in range(2)]
        rstds = [A.alloc("rstd", [T], F32)] * 2
        xrs = [alloc_parts(A, f"xr{i}", NJ, [T], BF16) for i in range(2)]
        mk = A.top
        ob, obb = alloc_parts(A, "o", NJ, [T], BF16)
        mk2 = A.top
        A.top = mk
        sq, sq_b = A.alloc("sq", [KD, T], BF16)
        A.top = mk2
        acc_s = [A.alloc(f"acc{i}", [T], F32) for i in range(2)]
        NS = 4
        NG = 8
        gg_s = [A.alloc(f"gg{i}", [T], BF16) for i in range(NG)]
        ra_s = [A.alloc(f"ra{i}", [T], F32) for i in range(NG)]
        tu_s = [A.alloc(f"tu{i}", [T], F32) for i in range(NG)]
        s2_s = [A.alloc(f"s2{i}", [T], F32) for i in range(NS)]
        y_s = [A.alloc(f"y{i}", [T], F32) for i in range(2)]
        cw = lambda j, i: recc[:, j, i:i + 1]
        cnt = {"in": 0, "gt": 0, "out": 0}

        def norm(t):
            norm_part1(0, t, sq, sq_b, rstds[t % 2][0], rstds[t % 2][1], 2)
            norm_part2(0, t, xns[t % 2][0], xns[t % 2][1], rstds[t % 2][0], rstds[t % 2][1])

        def step_b(t, j):
            par = t % 2
            xn, xn_b = xns[t % 2]
            xr, xrb = xrs[t % 2]
            n_in = cnt["in"]
            cnt["in"] += 1
            wsl, wsl_b = win_s[n_in % NWS]
            wload("w_in", wb["w_in"][j].rearrange("p (a b) -> p a b", a=KD), wsl, wsl_b, f"win{n_in % NWS}")
            bk, bkb = banks[n_in % 2]
            acc, acc_b = acc_s[n_in % 2]
            for k in range(KD):
                mm(bk, wsl[:, k, :], xn[:, k, :], k == 0, k == KD - 1, [wsl_b, xn_b], [bkb])
            act(acc, bk, AF.Identity, [bkb, recc_b], [acc_b], bias=cw(j, 4), scale=cw(j, 3))
            cp("act", tails_r[:, par, j, 0:3], bk[:, T - 3:T], [bkb], [trb[par][j]])
            if t > 0:
                tl = tails_r[:, 1 - par, j, :]
                tlb = trb[1 - par][j]
                stt(acc[:, 0:3], tl[:, 0:3], cw(j, 0), acc[:, 0:3], ALU.mult, ALU.add, [tlb, acc_b, recc_b], [acc_b])
                stt(acc[:, 0:2], tl[:, 1:3], cw(j, 1), acc[:, 0:2], ALU.mult, ALU.add, [tlb, acc_b, recc_b], [acc_b])
                stt(acc[:, 0:1], tl[:, 2:3], cw(j, 2), acc[:, 0:1], ALU.mult, ALU.add, [tlb, acc_b, recc_b], [acc_b])
            stt(acc[:, 1:T], bk[:, 0:T - 1], cw(j, 2), acc[:, 1:T], ALU.mult, ALU.add, [bkb, acc_b, recc_b], [acc_b])
            stt(acc[:, 2:T], bk[:, 0:T - 2], cw(j, 1), acc[:, 2:T], ALU.mult, ALU.add, [bkb, acc_b, recc_b], [acc_b])
            stt(xr[:, j, 3:T], bk[:, 0:T - 3], cw(j, 0), acc[:, 3:T], ALU.mult, ALU.add, [bkb, acc_b, recc_b], [xrb[j]])
            cp("dve", xr[:, j, 0:3], acc[:, 0:3], [acc_b], [xrb[j]])

        def gslot(t, j):
            return (t * NJ + j) % NG if False else ((j // NS + t * 3) % 2) * NS + j % NS

        def step_c(t, grp):
            xn, xn_b = xns[t % 2]
            xr, xrb = xrs[t % 2]
            for j in grp:
                n_gt = cnt["gt"]
                cnt["gt"] += 1
                wsl, wsl_b = wgt_s[n_gt % NWS]
                wload("w_gate", wb["w_gate"][j].rearrange("p (a b) -> p a b", a=KD), wsl, wsl_b, f"wgt{n_gt % NWS}")
                i2 = n_gt % 2
                bA, bAb = banks[2 + i2]
                bB, bBb = banks[4 + i2]
                bC, bCb = banks[6 + i2]
                for k in range(KD):
                    mm(bA, wsl[:, k, :], xn[:, k, :], k == 0, k == KD - 1, [wsl_b, xn_b], [bAb])
                tl_j = [(ti, kk) for ti, (jj, kk) in enumerate(GT) if jj == j]
                for n, (ti, kk) in enumerate(tl_j):
                    mm(bB, wrg[:, ti, :], xr[:, kk, :], n == 0, n == len(tl_j) - 1, [wrg_b, xrb[kk]], [bBb])
                for n, (ti, kk) in enumerate(tl_j):
                    mm(bC, wig[:, ti, :], xr[:, kk, :], n == 0, n == len(tl_j) - 1, [wig_b, xrb[kk]], [bCb])
                sg = gslot(t, j)
                act(gg_s[sg][0], bA, AF.Gelu_apprx_tanh, [bAb], [gg_s[sg][1]])
                act(ra_s[sg][0], bB, AF.Tanh, [bBb, rder_b], [ra_s[sg][1]], bias=rder[:, 1, j:j + 1], scale=0.5)
                act(tu_s[sg][0], bC, AF.Tanh, [bCb, rder_b], [tu_s[sg][1]], bias=rder[:, 2, j:j + 1], scale=0.5)
            for j in grp:
                tu, tu_b = tu_s[gslot(t, j)]
                stt(tu, tu, 1.0, xr[:, j, :], ALU.add, ALU.mult, [tu_b, xrb[j]], [tu_b])

        def step_c2(t, grp):
            xr, xrb = xrs[t % 2]
            for j in grp:
                sl = j % NS
                ra, ra_b = ra_s[gslot(t, j)]
                act(ra, ra, AF.Exp, [ra_b, rder_b], [ra_b], bias=rder[:, 0, j:j + 1], scale=rder[:, 0, j:j + 1])
                tt(s2_s[sl][0], ra, ra, ALU.mult, [ra_b], [s2_s[sl][1]], eng="pool")
            for j in grp:
                sl = j % NS
                s2, s2_b = s2_s[sl]
                act(s2, s2, AF.Sqrt, [s2_b, cst_b], [s2_b], bias=CQ, scale=-0.25)
            for j in grp:
                sl = j % NS
                tu, tu_b = tu_s[gslot(t, j)]
                s2, s2_b = s2_s[sl]
                tt(tu, tu, s2, ALU.mult, [tu_b, s2_b], [tu_b], eng="pool")
            for j in grp:
                ra, ra_b = ra_s[gslot(t, j)]
                tu, tu_b = tu_s[gslot(t, j)]
                yy, yy_b = y_s[j % 2]
                if t == 0:
                    P.op("dve", lambda e, yy=yy, ra=ra, tu=tu: e.tensor_tensor_scan(
                        out=yy, data0=ra, data1=tu, initial=0.0, op0=ALU.mult, op1=ALU.add),
                        reads=[ra_b, tu_b], writes=[yy_b])
                else:
                    P.op("dve", lambda e, yy=yy, ra=ra, tu=tu, j=j: e.tensor_tensor_scan(
                        out=yy, data0=ra, data1=tu, initial=state[:, j:j + 1], op0=ALU.mult, op1=ALU.add),
                        reads=[ra_b, tu_b, stb[j]], writes=[yy_b])
                gg, gg_b = gg_s[gslot(t, j)]
                cp("pool", state[:, j:j + 1], yy[:, T - 1:T], [yy_b], [stb[j]])
                tt(ob[:, j, :], gg, yy, ALU.mult, [gg_b, yy_b], [obb[j]], eng="pool")

        def step_d(t):
            for m in range(KD):
                n_out = cnt["out"]
                cnt["out"] += 1
                wsl, wsl_b = wout_s[n_out % 2]
                wload("w_out", wb["w_out"][m].rearrange("p (a b) -> p a b", a=NJ), wsl, wsl_b, f"wout{n_out % 2}")
                bk, bkb = banks[n_out % 2]
                for k in range(NJ):
                    mm(bk, wsl[:, k, :], ob[:, k, :], k == 0, k == NJ - 1, [wsl_b, obb[k]], [bkb])
                tt(h[:, m, tsl(t)], bk, h[:, m, tsl(t)], ALU.add, [bkb, hb[m][t]], [hb[m][t]])

        norm(0)
        for j in range(NJ):
            step_b(0, j)
        for t in range(NT):
            if t + 1 < NT:
                norm(t + 1)
            for j0 in range(0, NJ, NS):
                grp = list(range(j0, min(j0 + NS, NJ)))
                step_c(t, grp)
                drip(2)
                if t + 1 < NT:
                    for j in grp:
                        step_b(t + 1, j)
                step_c2(t, grp)
            step_d(t)

    def phase_F(s, l):
        A.reset(m0)
        A.limit = None
        gi = 1 if l == 0 else 4
        wn_up, wn_dn = f"w_up{l}", f"w_down{l}"
        wd, wd_b = A.alloc("wd", [NPAIR, D], BF16)
        wdv = wb[wn_dn].rearrange("p (a b) -> p a b", a=NPAIR)
        for q in range(4):
            dma("sp", wd[:, q * 6:(q + 1) * 6, :], wdv[:, q * 6:(q + 1) * 6, :], [wdram_b[wn_dn]], [wd_b], "wd")
        wup_s = [A.alloc(f"wup{i}", [KD, 2, 128], BF16) for i in range(3)]
        xn, xn_b = A.alloc("xn", [KD, T], BF16)
        sq, sq_b = A.alloc("sq", [KD, T], BF16)
        rstd, rstd_b = A.alloc("rstd", [T], F32)
        actbs = [alloc_parts(A, f"actb{i}", NPAIR, [T], BF16) for i in range(2)]
        ag_s = [A.alloc(f"ag{i}", [T], F32) for i in range(2)]
        av_s = [A.alloc(f"av{i}", [T], F32) for i in range(2)]
        hal, hal_b = A.alloc("hal", [48, 2], F32)
        hal2, _ = A.alloc("hal2", [48], F32, track=False)
        fw = lambda ch, i: ffnc[:, l, ch, i:i + 1]
        fwa = lambda i: ffnc[:, l, :, i]
        n_up = 0
        n_dn = [0]
        norm_part1(gi, 0, sq, sq_b, rstd, rstd_b, 6)
        norm_part2(gi, 0, xn, xn_b, rstd, rstd_b)

        def down_m(t, m):
            actb, actbb = actbs[t % 2]
            bk, bkb = banks[4 + n_dn[0] % 2]
            n_dn[0] += 1
            for k in range(NPAIR):
                mm(bk, wd[:, k, m * 128:(m + 1) * 128], actb[:, k, :], k == 0, k == NPAIR - 1, [wd_b, actbb[k]], [bkb])
            tt(h[:, m, tsl(t)], bk, h[:, m, tsl(t)], ALU.add, [bkb, hb[m][t]], [hb[m][t]])

        def stage_b(t, c, i2):
            actb, actbb = actbs[t % 2]
            ag, ag_b = ag_s[i2]
            av, av_b = av_s[i2]
            act(ag, ag, AF.Gelu_apprx_tanh, [ag_b], [ag_b])
            tt(actb[:, c, :], ag, av, ALU.mult, [ag_b, av_b], [actbb[c]], eng="pool")

        for t in range(NT):
            par = t % 2
            if t > 0:
                tlb = [tfb[1 - par][ch] for ch in range(48)]
                T0 = tails_f[:, 1 - par, :, 0]
                T1 = tails_f[:, 1 - par, :, 1]
                tt(hal[:, :, 1], T1, fwa(0), ALU.mult, tlb + [ffnc_b], [hal_b], eng="pool")
                tt(hal[:, :, 0], T1, fwa(1), ALU.mult, tlb + [ffnc_b], [hal_b], eng="pool")
                tt(hal2, T0, fwa(0), ALU.mult, tlb + [ffnc_b], [hal_b], eng="pool")
                tt(hal[:, :, 0], hal[:, :, 0], hal2, ALU.add, [hal_b], [hal_b], eng="pool")
            prev = None
            for c in range(NPAIR):
                wsl, wsl_b = wup_s[n_up % 3]
                wload(wn_up, wb[wn_up][c].rearrange("p (a g b) -> p a g b", a=KD, g=2), wsl, wsl_b, f"wup{n_up % 3}")
                i2 = n_up % 2
                n_up += 1
                for g in range(2):
                    ch = g * NPAIR + c
                    bk, bkb = banks[2 * g + i2]
                    acc, acc_b = (ag_s if g == 0 else av_s)[i2]
                    for k in range(KD):
                        mm(bk, wsl[:, k, g, :], xn[:, k, :], k == 0, k == KD - 1, [wsl_b, xn_b], [bkb])
                    act(acc, bk, AF.Identity, [bkb, ffnc_b], [acc_b], bias=fw(ch, 3), scale=fw(ch, 2))
                    if t > 0:
                        tt(acc[:, 0:2], acc[:, 0:2], hal[:, ch, :], ALU.add, [acc_b, hal_b], [acc_b], eng="pool")
                    cp("act", tails_f[:, par, ch, :], bk[:, T - 2:T], [bkb], [tfb[par][ch]])
                    stt(acc[:, 1:T], bk[:, 0:T - 1], fw(ch, 1), acc[:, 1:T], ALU.mult, ALU.add, [bkb, acc_b, ffnc_b], [acc_b])
                    stt(acc[:, 2:T], bk[:, 0:T - 2], fw(ch, 0), acc[:, 2:T], ALU.mult, ALU.add, [bkb, acc_b, ffnc_b], [acc_b])
                if prev is not None:
                    stage_b(t, *prev)
                prev = (c, i2)
                if c % 4 == 3:
                    drip(1)
                if t > 0 and c % 3 == 2:
                    down_m(t - 1, c // 3)
                if c == NPAIR - 6 and t + 1 < NT:
                    norm_part1(gi, t + 1, sq, sq_b, rstd, rstd_b, 6)
            stage_b(t, *prev)
            if t + 1 < NT:
                norm_part2(gi, t + 1, xn, xn_b, rstd, rstd_b)
        for m in range(KD):
            down_m(NT - 1, m)

    def rope_alloc(share=None):
        if share is None:
            posi, posi_b = A.alloc("posi", [T], I32)
            ang, ang_b = A.alloc("ang", [T], F32)
            zz, zz_b = A.alloc("zz", [T], F32)
            zi, zi_b = posi, posi_b
            zf, zf_b = A.alloc("zf", [T], F32)
        else:
            (posi, posi_b, ang, ang_b, zz, zz_b, zi, zi_b, zf, zf_b) = share[4:]
        Ct, Ct_b = A.alloc("Ct", [T], BF16)
        St, St_b = A.alloc("St", [T], F32)
        return (Ct, Ct_b, St, St_b, posi, posi_b, ang, ang_b, zz, zz_b, zi, zi_b, zf, zf_b)

    def rope_tables(s, t, reuse=None):
        if reuse is None:
            posi, posi_b = A.alloc("posi", [T], I32)
            ang, ang_b = A.alloc("ang", [T], F32)
            zz, zz_b = A.alloc("zz", [T], F32)
            zi, zi_b = A.alloc("zi", [T], I32)
            zf, zf_b = A.alloc("zf", [T], F32)
            Ct, Ct_b = A.alloc("Ct", [T], BF16)
            St, St_b = A.alloc("St", [T], F32)
        else:
            (Ct, Ct_b, St, St_b, posi, posi_b, ang, ang_b, zz, zz_b, zi, zi_b, zf, zf_b) = reuse
        dma("sp", posi, pos_d[s:s + 1, tsl(t)].broadcast_to([128, T]), [], [posi_b], f"posi{t % 2}")
        cp("dve", ang, posi, [posi_b], [ang_b])
        ts(ang, ang, invf[:, 0:1], None, ALU.mult, ALU.bypass, [ang_b, invf_b], [ang_b])
        for (dst, dst_b, off) in ((St, St_b, 0.5), (Ct, Ct_b, 0.75)):
            ts(zz, ang, 1.0 / TWO_PI, off, ALU.mult, ALU.add, [ang_b], [zz_b])
            cp("dve", zi, zz, [zz_b], [zi_b])
            cp("dve", zf, zi, [zi_b], [zf_b])
            tt(zz, zz, zf, ALU.subtract, [zz_b, zf_b], [zz_b])
            stt(zz, zz, 0.0, zz, ALU.is_lt, ALU.add, [zz_b], [zz_b])
            act(dst, zz, AF.Sin, [zz_b, cst_b], [dst_b], bias=CNPI, scale=TWO_PI)
        return (Ct, Ct_b, St, St_b, posi, posi_b, ang, ang_b, zz, zz_b, zi, zi_b, zf, zf_b)

    def hnr_tmps(n):
        return [dict(qsq=A.alloc("qsq", [T], BF16), rq=A.alloc("rq", [T], F32), qnb=A.alloc("qnb", [T], BF16),
                     t1=A.alloc("t1", [T], BF16), t2=A.alloc("t2", [T], BF16)) for _ in range(n)]

    def hnr_s1(src, tm):
        (src_bk, src_bkb) = src
        qsq, qsq_b = tm["qsq"]
        act(qsq, src_bk, AF.Square, [src_bkb], [qsq_b])

    def hnr_s2(src, gcol, tm, bank2):
        (src_bk, src_bkb) = src
        qsq, qsq_b = tm["qsq"]
        rq, rq_b = tm["rq"]
        qnb, qnb_b = tm["qnb"]
        b2, b2b = bank2
        mm(b2, bones_bf, qsq, True, True, [cmat_b, qsq_b], [b2b])
        act(rq, b2, AF.Ln, [b2b, cst_b], [rq_b], bias=CE, scale=1.0 / HD)
        act(rq, rq, AF.Exp, [rq_b], [rq_b], scale=-0.5)
        stt(qnb, src_bk, gcol, rq, ALU.mult, ALU.mult, [src_bkb, qkg_b, rq_b], [qnb_b])

    def hnr_s3(tm, bank3, Ct, Ct_b, St, St_b, dst, dst_bufs):
        qnb, qnb_b = tm["qnb"]
        t1, t1_b = tm["t1"]
        t2, t2_b = tm["t2"]
        b3, b3b = bank3
        mm(b3, rt_bf, qnb, True, True, [cmat_b, qnb_b], [b3b])
        tt(t1, qnb, Ct, ALU.mult, [qnb_b, Ct_b], [t1_b])
        tt(t2, b3, St, ALU.mult, [b3b, St_b], [t2_b])
        tt(dst, t1, t2, ALU.add, [t1_b, t2_b], dst_bufs)

    kvstate = {}

    def alloc_kv():
        save = A.top
        A.top = kv_lo
        A.in_kv = True
        Kf, Kfb = alloc_parts(A, "Kf", NT, [T], BF16)
        Kf = Kf.rearrange("p a b -> p (a b)")
        Vsb, Vb = alloc_parts(A, "Vsb", NT, [4, 128], BF16)
        Vsb = Vsb.rearrange("p a b c -> p (a b) c")
        esr, esr_b = A.alloc("esr", [NH, 128], BF16)
        A.in_kv = False
        A.top = save
        return Kf, Kfb, Vsb, Vb, esr, esr_b

    def phase_KA(s):
        A.reset(m0)
        A.limit = kv_lo
        Kf, Kfb, Vsb, Vb, esr, esr_b = alloc_kv()
        cp("dve", esr[0:1, :, :], sk[0:1, :].unsqueeze(2).broadcast_to([1, NH, 128]), [sk_b], [esr_b])
        wk, wk_b = A.alloc("wk", [KD, 128], BF16)
        wv, wv_b = A.alloc("wv", [KD, 128], BF16)
        wload("w_k", wb["w_k"].rearrange("p (a b) -> p a b", a=KD), wk, wk_b, "wk")
        wload("w_v", wb["w_v"].rearrange("p (a b) -> p a b", a=KD), wv, wv_b, "wv")
        wq, wq_b = A.alloc("wq", [KD, D], BF16)
        wqv = wb["w_q"].rearrange("p (a b) -> p a b", a=KD)
        for q in range(2):
            dma("sp", wq[:, q * 4:(q + 1) * 4, :], wqv[:, q * 4:(q + 1) * 4, :], [wdram_b["w_q"]], [wq_b], "wq")
        wo, wo_b = A.alloc("wo", [KD, D], BF16)
        wov = wb["w_o"].rearrange("p (a b) -> p a b", a=NH)
        for hf in range(2):
            dma("sp", wo[hf * 64:(hf + 1) * 64, :, :], wov[:, hf:NH:2, :], [wdram_b["w_o"]], [wo_b], "wo")
        sqk, sqk_b = A.alloc("sqk", [KD, T], BF16)
        xnk, xnk_b = A.alloc("xnk", [KD, T], BF16)
        rstdk, rstdk_b = A.alloc("rstdk", [T], F32)
        xn, xn_b = A.alloc("xn", [KD, T], BF16)
        rstd, rstd_b = rstdk, rstdk_b
        qrot, qrb = alloc_parts(A, "qrot", KD, [T], BF16)
        attT, atb = alloc_parts(A, "attT", KD, [T], BF16)
        p_s = [A.alloc(f"pT{i}", [T], BF16) for i in range(3)]
        rec_s = [A.alloc(f"rec{i}", [256], F32) for i in range(2)]
        tms = hnr_tmps(4)
        tmk = hnr_tmps(1)[0]
        ropes = [rope_alloc()]
        ropes.append(rope_alloc(share=ropes[0]))
        n_o = 0
        n_w = 0
        BK, BV = banks[6], banks[7]

        def k_thunks(t):
            th = []
            th.append(lambda: rope_tables(s, t, reuse=ropes[t % 2]))
            th.append(lambda: act(sqk, h[:, :, tsl(t)], AF.Square, [hb[k][t] for k in range(KD)], [sqk_b]))

            def n1():
                for k in range(KD):
                    mm(BV[0], ones_bf, sqk[:, k, :], k == 0, k == KD - 1, [sqk_b, cmat_b], [BV[1]])
                act(rstdk, BV[0], AF.Ln, [BV[1], cst_b], [rstdk_b], bias=CE, scale=1.0 / D)
                act(rstdk, rstdk, AF.Exp, [rstdk_b], [rstdk_b], scale=-0.5)
            th.append(n1)
            for k0 in range(0, KD, 2):
                def n2(k0=k0):
                    for k in (k0, k0 + 1):
                        stt(xnk[:, k, :], h[:, k, tsl(t)], gains[:, 2, k:k + 1], rstdk, ALU.mult, ALU.mult,
                            [hb[k][t], gains_b, rstdk_b], [xnk_b])
                th.append(n2)

            def kp():
                for k in range(KD):
                    mm(BK[0], wk[:, k, :], xnk[:, k, :], k == 0, k == KD - 1, [wk_b, xnk_b], [BK[1]])
                hnr_s1(BK, tmk)
            th.append(kp)

            def vp():
                for blk in range(4):
                    for k in range(KD):
                        mm(BV[0][:, blk * 128:(blk + 1) * 128], xnk[:, k, blk * 128:(blk + 1) * 128], wv[:, k, :],
                           k == 0, k == KD - 1, [wv_b, xnk_b], [BV[1]])
                cp("act", Vsb[:, t * 4:(t + 1) * 4, :], BV[0].rearrange("p (a b) -> p a b", a=4), [BV[1]], [Vb[t]])
            th.append(vp)
            th.append(lambda: hnr_s2(BK, qkg[:, 1:2], tmk, BV))
            th.append(lambda: hnr_s3(tmk, BV, ropes[t % 2][0], ropes[t % 2][1], ropes[t % 2][2], ropes[t % 2][3],
                                     Kf[:, tsl(t)], [Kfb[t]]))
            return th

        for f in k_thunks(0):
            f()
        for t in range(NT):
            norm_part1(3, t, sqk, sqk_b, rstd, rstd_b, 6)
            norm_part2(3, t, xn, xn_b, rstd, rstd_b)
            Ct, Ct_b, St, St_b = ropes[t % 2][:4]
            kth = k_thunks(t + 1) if t + 1 < NT else []
            if SEQ_K:
                while kth:
                    kth.pop(0)()
            for g4 in range(4):
                bo = 4 * (g4 % 2)
                tmq = [tms[(2 * g4 + u) % 4] for u in range(2)]
                cs = [2 * g4, 2 * g4 + 1]
                for u, c in enumerate(cs):
                    bQ = banks[bo + u]
                    for k in range(KD):
                        mm(bQ[0], wq[:, k, c * 128:(c + 1) * 128], xn[:, k, :], k == 0, k == KD - 1, [wq_b, xn_b], [bQ[1]])
                for u, c in enumerate(cs):
                    hnr_s1(banks[bo + u], tmq[u])
                for u, c in enumerate(cs):
                    (src_bk, src_bkb) = banks[bo + u]
                    qsq, qsq_b = tmq[u]["qsq"]
                    b2, b2b = banks[bo + 2 + u]
                    mm(b2, bones_bf, qsq, True, True, [cmat_b, qsq_b], [b2b])
                for u, c in enumerate(cs):
                    rq, rq_b = tmq[u]["rq"]
                    b2, b2b = banks[bo + 2 + u]
                    act(rq, b2, AF.Ln, [b2b, cst_b], [rq_b], bias=CE, scale=1.0 / HD)
                for u, c in enumerate(cs):
                    rq, rq_b = tmq[u]["rq"]
                    act(rq, rq, AF.Exp, [rq_b], [rq_b], scale=-0.5)
                for u, c in enumerate(cs):
                    (src_bk, src_bkb) = banks[bo + u]
                    rq, rq_b = tmq[u]["rq"]
                    qnb, qnb_b = tmq[u]["qnb"]
                    stt(qnb, src_bk, qkg[:, 0:1], rq, ALU.mult, ALU.mult, [src_bkb, qkg_b, rq_b], [qnb_b])
                for u, c in enumerate(cs):
                    qnb, qnb_b = tmq[u]["qnb"]
                    b3, b3b = banks[bo + 2 + u]
                    mm(b3, rt_bf, qnb, True, True, [cmat_b, qnb_b], [b3b])
                for u, c in enumerate(cs):
                    qnb, qnb_b = tmq[u]["qnb"]
                    t1, t1_b = tmq[u]["t1"]
                    tt(t1, qnb, Ct, ALU.mult, [qnb_b, Ct_b], [t1_b])
                for u, c in enumerate(cs):
                    t2, t2_b = tmq[u]["t2"]
                    b3, b3b = banks[bo + 2 + u]
                    tt(t2, b3, St, ALU.mult, [b3b, St_b], [t2_b])
                for u, c in enumerate(cs):
                    t1, t1_b = tmq[u]["t1"]
                    t2, t2_b = tmq[u]["t2"]
                    tt(qrot[:, c, :], t1, t2, ALU.add, [t1_b, t2_b], [qrb[c]])
            jobs = []
            for i in range(4):
                nb = t * 4 + i
                kbs = [kb for kb in (nb - 1, nb) if kb >= 0]
                for g in range(2):
                    for hg in range(2):
                        for n, kb in enumerate(kbs):
                            jobs.append((i, nb, g, hg, n, kb, len(kbs)))

            def emit_S(jn):
                i, nb, g, hg, n, kb, nk = jobs[jn]
                ps = slice(g * 64, (g + 1) * 64)
                bS, bSb = banks[jn % 3]
                mm(bS, Kf[ps, kb * 128:(kb + 1) * 128], qrot[ps, 4 * hg:4 * hg + 4, i * 128:(i + 1) * 128],
                   True, True, [Kfb[kb // 4]] + [qrb[4 * hg + u] for u in range(4)], [bSb])

            LA = 2
            pending = []
            for jn in range(min(LA, len(jobs))):
                emit_S(jn)
            for jn, (i, nb, g, hg, n, kb, nk) in enumerate(jobs):
                ps = slice(g * 64, (g + 1) * 64)
                bS, bSb = banks[jn % 3]
                pT, pT_b = p_s[jn % 3]
                act(pT, bS, AF.Exp, [bSb], [pT_b], scale=float(HD) ** -0.5)
                pT3 = pT.rearrange("p (a b) -> p a b", a=4)
                if kb == nb:
                    P.op("pool", lambda e, pT3=pT3: e.affine_select(
                        out=pT3, in_=pT3, pattern=[[0, 4], [1, 128]], compare_op=ALU.is_ge, fill=0.0,
                        base=0, channel_multiplier=-1), reads=[pT_b], writes=[pT_b])
                else:
                    P.op("pool", lambda e, pT3=pT3: e.affine_select(
                        out=pT3, in_=pT3, pattern=[[0, 4], [-1, 128]], compare_op=ALU.is_gt, fill=0.0,
                        base=0, channel_multiplier=1), reads=[pT_b], writes=[pT_b])
                if jn + LA < len(jobs):
                    emit_S(jn + LA)
                while pending and pending[0][0] <= jn:
                    pending.pop(0)[1]()
                if kth and jn % 2 == 1:
                    kth.pop(0)()
                if n == 0:
                    n_o += 1
                bOD, bODb = banks[3 + n_o % 3]
                h0 = 8 * g + 4 * hg
                for par_ in range(2):
                    po = slice(par_ * 64, (par_ + 1) * 64)
                    P.op("pe", lambda e, o=bOD[po, 0:256], l=Vsb[:, kb, ps], r=pT3[:, par_:4:2, :], st=(n == 0):
                         e.matmul(o, l, r, start=st, stop=False, skip_group_check=True),
                         reads=[Vb[kb // 4], pT_b], writes=[bODb])
                    P.op("pe", lambda e, o=bOD[po, 256:512], l=ones_bf[:, 0:64], r=pT3[:, par_:4:2, :]:
                         e.matmul(o, l, r, start=False, stop=False, skip_group_check=True),
                         reads=[cmat_b, pT_b], writes=[bODb])
                if n == nk - 1:
                    for par_ in range(2):
                        po = slice(par_ * 64, (par_ + 1) * 64)
                        P.op("pe", lambda e, o=bOD[po, 256:512], l=ones_bf[0:1, 0:64], r=esr[0:1, h0 + par_:h0 + 4:2, :]:
                             e.matmul(o, l, r, start=False, stop=True, skip_group_check=True),
                             reads=[cmat_b, esr_b], writes=[bODb])

                    def fin(rec_t=rec_s[n_o % 2], bOD=bOD, bODb=bODb, pr0=h0 // 2, i=i):
                        rec, rec_b = rec_t
                        act(rec, bOD[:, 256:512], AF.Ln, [bODb], [rec_b])
                        act(rec, rec, AF.Exp, [rec_b], [rec_b], scale=-1.0)
                        tt(attT[:, pr0:pr0 + 2, i * 128:(i + 1) * 128],
                           bOD[:, 0:256].rearrange("p (a b) -> p a b", a=2), rec.rearrange("p (a b) -> p a b", a=2),
                           ALU.mult, [bODb, rec_b], [atb[pr0], atb[pr0 + 1]])
                    pending.append((jn + nk, fin))
            while pending:
                pending.pop(0)[1]()
            while kth:
                kth.pop(0)()
            for m in range(KD):
                bk, bkb = banks[1 if n_w % 2 else 0]
                n_w += 1
                for c in range(KD):
                    mm(bk, wo[:, c, m * 128:(m + 1) * 128], attT[:, c, :], c == 0, c == KD - 1, [wo_b, atb[c]], [bkb])
                tt(h[:, m, tsl(t)], bk, h[:, m, tsl(t)], ALU.add, [bkb, hb[m][t]], [hb[m][t]])


    def phase_X(s):
        A.reset(m0)
        xin = [A.alloc(f"xin{i}", [D], F32) for i in range(4)]
        for blk in range(S // 128):
            xt, xb_ = xin[blk % 4]
            dma("sp", xt, x_d[s, blk * 128:(blk + 1) * 128, :], [], [xb_], f"xin{blk % 4}")
            t = blk // 4
            for half in range(2):
                bk, bkb = banks[(blk * 2 + half) % 8]
                for q in range(4):
                    k = half * 4 + q
                    P.op("pe", lambda e, o=bk[:, q * 128:(q + 1) * 128], i=xt[:, k * 128:(k + 1) * 128]:
                         e.transpose(out=o, in_=i, identity=identf), reads=[xb_, identf_b], writes=[bkb])
                eng = "act" if half == 0 else "dve"
                cp(eng, h[:, half * 4:half * 4 + 4, blk * 128:(blk + 1) * 128],
                   bk.rearrange("p (a b) -> p a b", a=4), [bkb], [hb[k][t] for k in range(half * 4, half * 4 + 4)])

    def phase_O(s):
        A.reset(m0)
        yo = [alloc_parts(A, f"yo{i}", 2, [512], F32) for i in range(4)]
        for blk in range(S // 128):
            yt, ybs = yo[blk % 4]
            t = blk // 4
            for half in range(2):
                bk, bkb = banks[(blk * 2 + half) % 8]
                for q in range(4):
                    k = half * 4 + q
                    P.op("pe", lambda e, o=bk[:, q * 128:(q + 1) * 128], i=h[:, k, blk * 128:(blk + 1) * 128]:
                         e.transpose(out=o, in_=i, identity=identf), reads=[hb[k][t], identf_b], writes=[bkb])
                eng = "act" if half == 0 else "dve"
                cp(eng, yt[:, half, :], bk, [bkb], [ybs[half]])
            dma("sp", y_d[s, blk * 128:(blk + 1) * 128, :], yt.rearrange("p a b -> p (a b)"), list(ybs), [], f"yo{blk % 4}")

    for s in range(nseq):
        phase_X(s)
        if s == 0:
            for n in ("w_up0", "w_down0", "w_k", "w_v", "w_q", "w_o", "w_up1", "w_down1"):
                cast_weight(n, defer=True)
        if last_stage >= 1:
            phase_R(s)
        flush_casts(("w_up0", "w_down0"))
        if last_stage >= 2:
            phase_F(s, 0)
        flush_casts(("w_k", "w_v", "w_q", "w_o", "w_up1", "w_down1"))
        if last_stage >= 4:
            phase_KA(s)
        if last_stage >= 5:
            phase_F(s, 1)
        phase_O(s)

    final_ops = [o for o in P.q["sp"] if o.is_dma]
    engines = {"pe": nc.tensor, "act": nc.scalar, "dve": nc.vector, "pool": nc.gpsimd, "sp": nc.sync}
    P.op("sp", lambda e: e.nop(), extra_deps=final_ops)
    P.op("pool", lambda e: e.nop(), extra_deps=[o for o in P.q["pool"] if o.is_dma])

    with nc.Block() as block:
        def run(ename):
            def body(eng):
                Pn = P
                waited = {}
                for o in Pn.q[ename]:
                    need = {}
                    for d in o.deps:
                        kk = id(d.sem)
                        if kk not in need or need[kk][1] < d.val:
                            need[kk] = (d.sem, d.val)
                    for kk, (sem, val) in need.items():
                        if waited.get(kk, 0) < val:
                            eng.wait_ge(sem, val)
                            waited[kk] = val
                    inst = o.fn(eng)
                    if o.signal and inst is not None:
                        inst.then_inc(o.sem, 16 if o.is_dma else 1)
            return body

        SEG = 12000
        for e in ENGS:
            cnt = 0
            si = 0
            for o in P.q[e]:
                if o.is_dma:
                    c = P.dma_counts.get(id(o.dsem), 0) + 16
                    P.dma_counts[id(o.dsem)] = c
                    o.sem, o.val = o.dsem, c
                elif o.signal:
                    if cnt >= SEG:
                        si += 1
                        cnt = 0
                    cnt += 1
                    o.sem, o.val = sems[e][si], cnt
        block.tensor(run("pe"))
        block.scalar(run("act"))
        block.vector(run("dve"))
        block.gpsimd(run("pool"))
        block.sync(run("sp"))
    es.close()
    return nc


def _img(w, kc):
    K, N = w.shape
    assert K == kc * 128
    return np.ascontiguousarray(w.reshape(kc, 128, N).transpose(1, 0, 2).reshape(128, kc * N))


def prepare_shared(inp):
    f = np.float32
    o = {}
    pad = LWP - LW
    w_in = np.pad(np.asarray(inp["a_w_in"][0], f), ((0, 0), (0, pad)))
    w_gate = np.pad(np.asarray(inp["a_w_gate"][0], f), ((0, 0), (0, pad)))

    def chunked_cols(w):
        return np.ascontiguousarray(w.reshape(KD, 128, NJ, 128).transpose(2, 1, 0, 3).reshape(NJ, 128, KD * 128))
    o["w_in"] = chunked_cols(w_in)
    o["w_gate"] = chunked_cols(w_gate)
    for nm, key in (("w_rg", "a_w_rg"), ("w_ig", "a_w_ig")):
        wbd = np.zeros((LWP, LWP), f)
        blk = np.asarray(inp[key][0], f)
        for b in range(8):
            wbd[b * LBLK:(b + 1) * LBLK, b * LBLK:(b + 1) * LBLK] = blk[b]
        tiles = np.stack([wbd[kk * 128:(kk + 1) * 128, j * 128:(j + 1) * 128] for (j, kk) in GT], 0)
        o[nm] = np.ascontiguousarray(tiles.transpose(1, 0, 2).reshape(128, NGT * 128))
    w_out = np.pad(np.asarray(inp["a_w_out"][0], f), ((0, pad), (0, 0)))
    o["w_out"] = np.ascontiguousarray(w_out.reshape(NJ, 128, KD, 128).transpose(2, 1, 0, 3).reshape(KD, 128, NJ * 128))
    for l in range(2):
        wu = np.asarray(inp["f_w_up"][l], f)
        o[f"w_up{l}"] = np.ascontiguousarray(
            wu.reshape(KD, 128, 2, NPAIR, 128).transpose(3, 1, 0, 2, 4).reshape(NPAIR, 128, KD * 2 * 128))
        o[f"w_down{l}"] = _img(np.asarray(inp["f_w_down"][l], f), NPAIR)
    o["w_k"] = _img(np.asarray(inp["w_k"], f), KD)
    o["w_v"] = _img(np.asarray(inp["w_v"], f), KD)
    wq = np.asarray(inp["b_w_q"][0], f)
    perm = np.concatenate([np.concatenate([np.arange(c * 64, c * 64 + 64), np.arange((8 + c) * 64, (8 + c) * 64 + 64)])
                           for c in range(8)])
    o["w_q"] = _img(np.ascontiguousarray(wq[:, perm]), KD)
    wo = np.asarray(inp["b_w_o"][0], f)
    o["w_o"] = np.ascontiguousarray(wo.reshape(NH, 64, D).transpose(1, 0, 2).reshape(64, NH * D))

    def vimg(v, kc):
        return np.asarray(v, f).reshape(kc, 128).T
    gains = np.stack([vimg(inp["a_norm"][0], KD), vimg(inp["f_norm"][0], KD), vimg(inp["kv_norm"], KD),
                      vimg(inp["b_norm"][0], KD), vimg(inp["f_norm"][1], KD)], 1)
    o["gains"] = np.ascontiguousarray(gains.reshape(128, 5 * KD))

    def padv(v):
        return np.pad(np.asarray(v, f), (0, pad))
    cw = np.asarray(inp["a_conv_w"][0], f)
    rec = np.stack([vimg(padv(cw[0]), NJ), vimg(padv(cw[1]), NJ), vimg(padv(cw[2]), NJ), vimg(padv(cw[3]), NJ),
                    vimg(padv(inp["a_conv_b"][0]), NJ), vimg(padv(inp["a_b_rg"][0]), NJ),
                    vimg(padv(inp["a_b_ig"][0]), NJ), vimg(padv(inp["a_lam"][0]), NJ)], 2)
    o["rec_c"] = np.ascontiguousarray(rec.reshape(128, NJ * 8))
    ffn = np.zeros((128, 2, 48, 4), f)
    for l in range(2):
        fw = np.asarray(inp["f_conv_w"][l], f)
        for i in range(3):
            ffn[:, l, :, i] = vimg(fw[i], 48)
        ffn[:, l, :, 3] = vimg(inp["f_conv_b"][l], 48)
    o["ffn_c"] = np.ascontiguousarray(ffn.reshape(128, -1))
    qn = np.asarray(inp["q_norm"][0], f)
    kn = np.asarray(inp["k_norm"], f)
    o["qk_g"] = np.ascontiguousarray(np.stack([np.tile(qn, 2), np.tile(kn, 2)], 1))
    o["sinks_row"] = np.asarray(inp["sinks"][0], f).reshape(1, NH)
    o["ident_f"] = np.eye(128, dtype=f)
    ones = np.ones((128, 128), f)
    bones = np.zeros((128, 128), f)
    bones[:64, :64] = 1
    bones[64:, 64:] = 1
    R = np.zeros((128, 128), f)
    for hh in range(2):
        for d in range(8):
            R[hh * 64 + d, hh * 64 + d + 8] = -1
            R[hh * 64 + d + 8, hh * 64 + d] = 1
    o["cmat"] = np.ascontiguousarray(np.concatenate([ones, bones, R.T], 1))
    inv_freq = (ROPE_THETA ** (-np.arange(0, ROPE_DIM, 2, dtype=np.float32) / ROPE_DIM)).astype(f)
    iv = np.zeros((128, 1), f)
    for p in range(128):
        d = p % 64
        if d < ROPE_DIM:
            iv[p, 0] = inv_freq[d % 8]
    o["invf"] = iv
    return o


_CACHE = {}


def kernel(**inputs):
    stop_after = inputs.pop("_stop_after", "F1")
    ncores = inputs.pop("_ncores", NCORES)
    x = np.asarray(inputs["x"], np.float32)
    pos = np.asarray(inputs["positions"], np.int32)
    shared = prepare_shared(inputs)
    key = stop_after
    if key not in _CACHE:
        _CACHE[key] = build_program(stop_after)
    nc = _CACHE[key]
    in_maps = []
    for c in range(ncores):
        m = dict(shared)
        m["x"] = np.ascontiguousarray(x[c * NSEQ:(c + 1) * NSEQ])
        m["pos"] = np.ascontiguousarray(pos[c * NSEQ:(c + 1) * NSEQ])
        in_maps.append(m)
    res = run_bass_kernel_spmd(nc, in_maps, core_ids=list(range(ncores)))
    out = np.concatenate([np.asarray(r["y"]) for r in res.results], axis=0)
    return out.astype(np.float32)
```

```python
import numpy as np
import concourse.bass as bass
import concourse.mybir as mybir
from concourse.bass_utils import run_bass_kernel_spmd

F32 = mybir.dt.float32
BF16 = mybir.dt.bfloat16
I32 = mybir.dt.int32
AF = mybir.ActivationFunctionType
ALU = mybir.AluOpType

NCORES = 8
D = 1024
S = 2048
NSEQ = 2
T = 512
NT = S // T
KD = D // 128
LW = 1344
LWP = 1408
NJ = LWP // 128
LBLK = 168
FF = 3072
NPAIR = FF // 128
NH = 16
HD = 64
EPS = 1e-6
ROPE_DIM = 16
ROPE_THETA = 500000.0

ENGS = ("pe", "act", "dve", "pool", "sp")


class Buf:
    __slots__ = ("name", "space", "lo", "hi", "writers", "readers", "overl")

    def __init__(self, name, space, lo, hi):
        self.name, self.space, self.lo, self.hi = name, space, lo, hi
        self.writers = []
        self.readers = []
        self.overl = []


class Op:
    __slots__ = ("eng", "fn", "deps", "signal", "sem", "val", "is_dma", "dsem", "idx")

    def __init__(self, eng, fn, is_dma=False, dsem=None):
        self.eng, self.fn, self.is_dma, self.dsem = eng, fn, is_dma, dsem
        self.deps = []
        self.signal = False
        self.sem = None
        self.val = None


class Prog:
    def __init__(self, nc):
        self.nc = nc
        self.q = {e: [] for e in ENGS}
        self.bufs = {"sb": [], "ps": [], "dram": []}
        self.dma_counts = {}

    def buf(self, name, space, lo, hi):
        b = Buf(name, space, lo, hi)
        for o in self.bufs[space]:
            if o.lo < hi and lo < o.hi:
                o.overl.append(b)
                b.overl.append(o)
        self.bufs[space].append(b)
        return b

    def op(self, eng, fn, reads=(), writes=(), is_dma=False, dsem=None, extra_deps=()):
        o = Op(eng, fn, is_dma, dsem)
        deps = []
        for b in reads:
            deps.extend(b.writers)
            for ob in b.overl:
                deps.extend(ob.writers)
            if b.space == "ps":
                deps.extend(r for r in b.readers if r.eng != eng)
        for b in writes:
            deps.extend(b.writers)
            deps.extend(b.readers)
            for ob in b.overl:
                deps.extend(ob.writers)
                deps.extend(ob.readers)
        deps.extend(extra_deps)
        seen = set()
        for d in deps:
            if d is o or id(d) in seen:
                continue
            seen.add(id(d))
            if (not d.is_dma) and (not is_dma) and d.eng == "pe" and eng == "pe":
                continue
            o.deps.append(d)
            d.signal = True
        for b in writes:
            b.writers = [o]
            b.readers = []
            for ob in b.overl:
                ob.writers = [w for w in ob.writers]
                ob.readers = []
                ob.writers = [o]
        for b in reads:
            b.readers.append(o)
        if is_dma:
            o.signal = True
        self.q[eng].append(o)
        return o


class Arena:
    def __init__(self, P, ap, nwords):
        self.P, self.ap, self.n = P, ap, nwords
        self.top = 0
        self.cnt = 0

    def mark(self):
        return self.top

    def reset(self, m):
        self.top = m

    def alloc(self, name, shape, dtype, track=True):
        n = int(np.prod(shape))
        words = n if dtype in (F32, I32) else (n + 1) // 2
        words = (words + 7) // 8 * 8
        lo = self.top
        lim = getattr(self, "limit", None) or self.n
        assert lo + words <= lim or getattr(self, "in_kv", False), f"arena overflow allocating {name}: {lo}+{words} > {lim}"
        self.top += words
        v = self.ap[:, lo:lo + words]
        if dtype != F32:
            v = v.bitcast(dtype)
        v = v[:, 0:n]
        if len(shape) == 2:
            v = v.rearrange("p (a b) -> p a b", a=shape[0])
        elif len(shape) == 3:
            v = v.rearrange("p (a b c) -> p a b c", a=shape[0], b=shape[1])
        elif len(shape) == 4:
            v = v.rearrange("p (a b c d) -> p a b c d", a=shape[0], b=shape[1], c=shape[2])
        self.cnt += 1
        b = self.P.buf(f"{name}#{self.cnt}", "sb", lo * 4, (lo + words) * 4) if track else None
        return v, b


def alloc_parts(A, name, nparts, part_shape, dtype):
    n = int(np.prod(part_shape))
    esz = 4 if dtype in (F32, I32) else 2
    lo = A.top
    v, _ = A.alloc(name, [nparts] + list(part_shape), dtype, track=False)
    bufs = [A.P.buf(f"{name}[{i}]#{A.cnt}", "sb", lo * 4 + i * n * esz, lo * 4 + (i + 1) * n * esz) for i in range(nparts)]
    return v, bufs


def gate_tile_list():
    tiles = []
    for j in range(NJ):
        c0, c1 = j * 128, min(j * 128 + 128, LW)
        if c0 >= LW:
            continue
        b0, b1 = c0 // LBLK, (c1 - 1) // LBLK
        r0, r1 = b0 * LBLK, (b1 + 1) * LBLK
        for kk in range(r0 // 128, (r1 - 1) // 128 + 1):
            tiles.append((j, kk))
    return tiles


GT = gate_tile_list()
NGT = len(GT)

STAGES = ("X", "R", "F0", "K", "A", "F1")
SEQ_K = False
early_out = True


def build_program(stop_after="F1", nseq=NSEQ):
    nc = bass.Bass("TRN2", target_bir_lowering=False)
    P = Prog(nc)
    last_stage = STAGES.index(stop_after)

    def dram_in(name, shape, dt=F32):
        return nc.dram_tensor(name, list(shape), dt, kind="ExternalInput").ap()

    def dram_scratch(name, shape, dt=BF16):
        return nc.dram_tensor(name, list(shape), dt, kind="Internal").ap()

    x_d = dram_in("x", [NSEQ, S, D])
    pos_d = dram_in("pos", [NSEQ, S], I32)
    y_d = nc.dram_tensor("y", [NSEQ, S, D], F32, kind="ExternalOutput").ap()

    wspecs = [
        ("w_in", [NJ, 128, KD * 128]),
        ("w_gate", [NJ, 128, KD * 128]),
        ("w_rg", [128, NGT * 128]),
        ("w_ig", [128, NGT * 128]),
        ("w_out", [KD, 128, NJ * 128]),
        ("w_up0", [NPAIR, 128, KD * 2 * 128]),
        ("w_down0", [128, NPAIR * D]),
        ("w_k", [128, KD * 128]),
        ("w_v", [128, KD * 128]),
        ("w_q", [128, KD * D]),
        ("w_o", [64, NH * D]),
        ("w_up1", [NPAIR, 128, KD * 2 * 128]),
        ("w_down1", [128, NPAIR * D]),
    ]
    wgroup = {"w_in": 0, "w_gate": 0, "w_rg": 0, "w_ig": 0, "w_out": 0, "w_up0": 1, "w_down0": 1,
              "w_k": 2, "w_v": 2, "w_q": 2, "w_o": 2, "w_up1": 3, "w_down1": 3}
    wf = {n: dram_in(n, s) for n, s in wspecs}
    wb = {n: dram_scratch(n + "_b", s) for n, s in wspecs}

    gains_d = dram_in("gains", [128, 5 * KD])
    recc_d = dram_in("rec_c", [128, NJ * 8])
    ffnc_d = dram_in("ffn_c", [128, 2 * 48 * 4])
    qkg_d = dram_in("qk_g", [128, 2])
    sinks_d = dram_in("sinks_row", [1, NH])
    identf_d = dram_in("ident_f", [128, 128])
    cmat_d = dram_in("cmat", [128, 3 * 128])
    invf_d = dram_in("invf", [128, 1])

    from contextlib import ExitStack
    es = ExitStack()
    ARENA_WORDS = 53200
    arena_t = es.enter_context(nc.sbuf_tensor("arena", [128, ARENA_WORDS], F32))
    psum_t = es.enter_context(nc.psum_tensor("psum", [128, 8 * 512], F32))
    A = Arena(P, arena_t[:], ARENA_WORDS)
    banks = []
    for b in range(8):
        banks.append((psum_t[:, b * 512:(b + 1) * 512], P.buf(f"bank{b}", "ps", b * 2048, (b + 1) * 2048)))

    nsem = {"pe": 3, "act": 4, "dve": 5, "pool": 2, "sp": 1}
    sems = {e: [es.enter_context(nc.semaphore(f"s_{e}{i}")) for i in range(n)] for e, n in nsem.items()}
    dma_sems = {}

    def dsem(name):
        if name not in dma_sems:
            dma_sems[name] = es.enter_context(nc.semaphore("d_" + name))
        return dma_sems[name]

    def dma(eng, out, in_, reads, writes, sem, **kw):
        return P.op(eng, lambda e: e.dma_start(out=out, in_=in_, **kw), reads=reads, writes=writes,
                    is_dma=True, dsem=dsem(sem))

    def mm(out, lhsT, rhs, start, stop, reads, writes):
        return P.op("pe", lambda e: e.matmul(out, lhsT, rhs, start=start, stop=stop), reads=reads, writes=writes)

    def act(out, in_, func, reads, writes, bias=None, scale=None, eng="act"):
        kw = {}
        if bias is not None:
            kw["bias"] = bias
        if scale is not None:
            kw["scale"] = scale
        return P.op(eng, lambda e: e.activation(out=out, in_=in_, func=func, **kw), reads=reads, writes=writes)

    def tt(out, in0, in1, op, reads, writes, eng="dve"):
        return P.op(eng, lambda e: e.tensor_tensor(out=out, in0=in0, in1=in1, op=op), reads=reads, writes=writes)

    def stt(out, in0, scalar, in1, op0, op1, reads, writes):
        return P.op("dve", lambda e: e.scalar_tensor_tensor(out=out, in0=in0, scalar=scalar, in1=in1, op0=op0, op1=op1),
                    reads=reads, writes=writes)

    def ts(out, in0, s1, s2, op0, op1, reads, writes, eng="dve"):
        return P.op(eng, lambda e: e.tensor_scalar(out=out, in0=in0, scalar1=s1, scalar2=s2, op0=op0, op1=op1),
                    reads=reads, writes=writes)

    def cp(eng, out, in_, reads, writes):
        if eng == "act":
            return P.op("act", lambda e: e.activation(out=out, in_=in_, func=AF.Copy), reads=reads, writes=writes)
        return P.op(eng, lambda e: e.tensor_copy(out=out, in_=in_), reads=reads, writes=writes)

    h, _ = A.alloc("h", [KD, S], F32, track=False)
    hb = [[P.buf(f"h{k}_{t}", "sb", (k * S + t * T) * 4, (k * S + t * T + T) * 4) for t in range(NT)]
          for k in range(KD)]
    identf, identf_b = A.alloc("identf", [128], F32)
    cmat, cmat_b = A.alloc("cmat", [3, 128], BF16)
    ones_bf, bones_bf, rt_bf = cmat[:, 0, :], cmat[:, 1, :], cmat[:, 2, :]
    gains, gains_b = A.alloc("gains", [5, KD], F32)
    recc, recc_b = A.alloc("recc", [NJ, 8], F32)
    ffnc, ffnc_b = A.alloc("ffnc", [2, 48, 4], F32)
    qkg, qkg_b = A.alloc("qkg", [2], F32)
    invf, invf_b = A.alloc("invf", [1], F32)

    dma("sp", identf, identf_d, [], [identf_b], "const1")
    dma("pool", cmat, cmat_d.rearrange("p (a b) -> p a b", a=3), [], [cmat_b], "constc")
    dma("sp", gains, gains_d.rearrange("p (a b) -> p a b", a=5), [], [gains_b], "const2")
    dma("sp", recc, recc_d.rearrange("p (a b) -> p a b", a=NJ), [], [recc_b], "const3")
    dma("sp", ffnc, ffnc_d.rearrange("p (a b c) -> p a b c", a=2, b=48), [], [ffnc_b], "const4")
    dma("sp", qkg, qkg_d, [], [qkg_b], "const5")
    dma("sp", invf, invf_d, [], [invf_b], "const6")

    wdram_b = {n: P.buf("wd_" + n, "dram", 0, 0) for n, _ in wspecs}
    wshape = dict(wspecs)

    def cast_weight(n, defer=False):
        sh = wshape[n]
        tot = int(np.prod(sh))
        b = wdram_b[n]
        src = wf[n]
        dst = wb[n]
        if len(sh) == 3:
            src = src.rearrange("a p f -> (a p f)")
            dst = dst.rearrange("a p f -> (a p f)")
        else:
            src = src.rearrange("p f -> (p f)")
            dst = dst.rearrange("p f -> (p f)")
        assert tot % 2048 == 0
        src = src.rearrange("(r c) -> r c", c=2048)
        dst = dst.rearrange("(r c) -> r c", c=2048)
        rows, cols = src.shape
        step = max(16, (1 << 20) // cols // 16 * 16)
        r = 0
        while r < rows:
            rr = min(step, rows - r)

            def issue(o=dst[r:r + rr, :], i=src[r:r + rr, :], b=b, n=n):
                P.op("pool", lambda e: e.dma_start(out=o, in_=i), writes=[b], is_dma=True, dsem=dsem(f"wc_{n}"))
            if defer:
                pending_casts.append((n, issue))
            else:
                issue()
            r += rr

    pending_casts = []

    def drip(k=1):
        for _ in range(k):
            if pending_casts:
                pending_casts.pop(0)[1]()

    def flush_casts(names):
        while any(n in names for n, _ in pending_casts):
            pending_casts.pop(0)[1]()

    for n in ("w_in", "w_gate", "w_rg", "w_ig", "w_out"):
        cast_weight(n)

    cst, cst_b = A.alloc("cst", [8], F32)
    CE, CQ, C1, CNPI = cst[:, 0:1], cst[:, 1:2], cst[:, 2:3], cst[:, 3:4]
    for col, val in ((0, EPS), (1, 0.25), (2, 1.0), (3, -float(np.pi))):
        P.op("dve", lambda e, c=col, v=val: e.memset(cst[:, c:c + 1], v), writes=[cst_b])
    rder, rder_b = A.alloc("rder", [3, NJ], F32)
    sk, sk_b = A.alloc("sk", [NH], F32)
    state, _ = A.alloc("state", [NJ], F32, track=False)
    stb = [P.buf(f"st{j}", "sb", 0, 0) for j in range(NJ)]
    tails_r, _ = A.alloc("tails_r", [2, NJ, 4], F32, track=False)
    trb = [[P.buf(f"tr{p}_{j}", "sb", 0, 0) for j in range(NJ)] for p in range(2)]
    tails_f, _ = A.alloc("tails_f", [2, 48, 2], F32, track=False)
    tfb = [[P.buf(f"tf{p}_{c}", "sb", 0, 0) for c in range(48)] for p in range(2)]

    lam = recc[:, :, 7]
    tmpc, tmpc_b = A.alloc("tmpc", [NJ], F32)
    act(tmpc, lam, AF.Exp, [recc_b], [tmpc_b], scale=-1.0)
    act(tmpc, tmpc, AF.Ln, [tmpc_b, cst_b], [tmpc_b], bias=C1)
    ts(rder[:, 0, :], tmpc, -4.0, None, ALU.mult, ALU.bypass, [tmpc_b], [rder_b])
    ts(rder[:, 1, :], recc[:, :, 5], 0.5, None, ALU.mult, ALU.bypass, [recc_b], [rder_b])
    ts(rder[:, 2, :], recc[:, :, 6], 0.5, None, ALU.mult, ALU.bypass, [recc_b], [rder_b])
    dma("sp", sk[0:1, :], sinks_d, [], [sk_b], "const_sk")
    act(sk[0:1, :], sk[0:1, :], AF.Exp, [sk_b], [sk_b])

    m0 = A.mark()
    print('persistent words', m0, 'phase budget KiB', (ARENA_WORDS - m0) * 4 / 1024)
    KV_WORDS = (S // 2) + (S // 2) + (NH * 128 // 2)
    kv_lo = ARENA_WORDS - KV_WORDS - 8
    TWO_PI = float(2 * np.pi)

    def tsl(t):
        return slice(t * T, (t + 1) * T)

    def recip(out, in_, reads, writes):
        return P.op("dve", lambda e: e.reciprocal_approx_fast(out=out, in_=in_), reads=reads, writes=writes)

    def norm_part1(gi, t, sq, sq_b, rstd, rstd_b, bankid):
        bk, bkb = banks[bankid]
        act(sq, h[:, :, tsl(t)], AF.Square, [hb[k][t] for k in range(KD)], [sq_b])
        for k in range(KD):
            mm(bk, ones_bf, sq[:, k, :], k == 0, k == KD - 1, [sq_b, cmat_b], [bkb])
        act(rstd, bk, AF.Ln, [bkb, cst_b], [rstd_b], bias=CE, scale=1.0 / D)
        act(rstd, rstd, AF.Exp, [rstd_b], [rstd_b], scale=-0.5)

    def norm_part2(gi, t, xn, xn_b, rstd, rstd_b):
        for k in range(KD):
            stt(xn[:, k, :], h[:, k, tsl(t)], gains[:, gi, k:k + 1], rstd, ALU.mult, ALU.mult,
                [hb[k][t], gains_b, rstd_b], [xn_b])

    def wload(name, unit_ap, slot, slot_b, semname):
        return dma("sp", slot, unit_ap, [wdram_b[name]], [slot_b], semname)

    def phase_R(s):
        A.reset(m0)
        A.limit = None
        wrg, wrg_b = A.alloc("wrg", [NGT, 128], BF16)
        wig, wig_b = A.alloc("wig", [NGT, 128], BF16)
        wload("w_rg", wb["w_rg"].rearrange("p (a b) -> p a b", a=NGT), wrg, wrg_b, "wrg")
        wload("w_ig", wb["w_ig"].rearrange("p (a b) -> p a b", a=NGT), wig, wig_b, "wig")
        NWS = 2
        win_s = [A.alloc(f"win{i}", [KD, 128], BF16) for i in range(NWS)]
        wgt_s = [A.alloc(f"wgt{i}", [KD, 128], BF16) for i in range(NWS)]
        wout_s = [A.alloc(f"wout{i}", [NJ, 128], BF16) for i in range(2)]
        xns = [A.alloc(f"xn{i}", [KD, T], BF16) for i in range(2)]
        rstds = [A.alloc("rstd", [T], F32)] * 2
        xrs = [alloc_parts(A, f"xr{i}", NJ, [T], BF16) for i in range(2)]
        mk = A.top
        ob, obb = alloc_parts(A, "o", NJ, [T], BF16)
        mk2 = A.top
        A.top = mk
        sq, sq_b = A.alloc("sq", [KD, T], BF16)
        A.top = mk2
        acc_s = [A.alloc(f"acc{i}", [T], F32) for i in range(2)]
        NS = 4
        NG = 8
        gg_s = [A.alloc(f"gg{i}", [T], BF16) for i in range(NG)]
        ra_s = [A.alloc(f"ra{i}", [T], F32) for i in range(NG)]
        tu_s = [A.alloc(f"tu{i}", [T], F32) for i in range(NG)]
        s2_s = [A.alloc(f"s2{i}", [T], F32) for i in range(NS)]
        y_s = [A.alloc(f"y{i}", [T], F32) for i in range(2)]
        cw = lambda j, i: recc[:, j, i:i + 1]
        cnt = {"in": 0, "gt": 0, "out": 0}

        def norm(t):
            norm_part1(0, t, sq, sq_b, rstds[t % 2][0], rstds[t % 2][1], 2)
            norm_part2(0, t, xns[t % 2][0], xns[t % 2][1], rstds[t % 2][0], rstds[t % 2][1])

        def step_b(t, j):
            par = t % 2
            xn, xn_b = xns[t % 2]
            xr, xrb = xrs[t % 2]
            n_in = cnt["in"]
            cnt["in"] += 1
            wsl, wsl_b = win_s[n_in % NWS]
            wload("w_in", wb["w_in"][j].rearrange("p (a b) -> p a b", a=KD), wsl, wsl_b, f"win{n_in % NWS}")
            bk, bkb = banks[n_in % 2]
            acc, acc_b = acc_s[n_in % 2]
            for k in range(KD):
                mm(bk, wsl[:, k, :], xn[:, k, :], k == 0, k == KD - 1, [wsl_b, xn_b], [bkb])
            act(acc, bk, AF.Identity, [bkb, recc_b], [acc_b], bias=cw(j, 4), scale=cw(j, 3))
            cp("act", tails_r[:, par, j, 0:3], bk[:, T - 3:T], [bkb], [trb[par][j]])
            if t > 0:
                tl = tails_r[:, 1 - par, j, :]
                tlb = trb[1 - par][j]
                stt(acc[:, 0:3], tl[:, 0:3], cw(j, 0), acc[:, 0:3], ALU.mult, ALU.add, [tlb, acc_b, recc_b], [acc_b])
                stt(acc[:, 0:2], tl[:, 1:3], cw(j, 1), acc[:, 0:2], ALU.mult, ALU.add, [tlb, acc_b, recc_b], [acc_b])
                stt(acc[:, 0:1], tl[:, 2:3], cw(j, 2), acc[:, 0:1], ALU.mult, ALU.add, [tlb, acc_b, recc_b], [acc_b])
            stt(acc[:, 1:T], bk[:, 0:T - 1], cw(j, 2), acc[:, 1:T], ALU.mult, ALU.add, [bkb, acc_b, recc_b], [acc_b])
            stt(acc[:, 2:T], bk[:, 0:T - 2], cw(j, 1), acc[:, 2:T], ALU.mult, ALU.add, [bkb, acc_b, recc_b], [acc_b])
            stt(xr[:, j, 3:T], bk[:, 0:T - 3], cw(j, 0), acc[:, 3:T], ALU.mult, ALU.add, [bkb, acc_b, recc_b], [xrb[j]])
            cp("dve", xr[:, j, 0:3], acc[:, 0:3], [acc_b], [xrb[j]])

        def gslot(t, j):
            return (t * NJ + j) % NG if False else ((j // NS + t * 3) % 2) * NS + j % NS

        def step_c(t, grp):
            xn, xn_b = xns[t % 2]
            xr, xrb = xrs[t % 2]
            for j in grp:
                n_gt = cnt["gt"]
                cnt["gt"] += 1
                wsl, wsl_b = wgt_s[n_gt % NWS]
                wload("w_gate", wb["w_gate"][j].rearrange("p (a b) -> p a b", a=KD), wsl, wsl_b, f"wgt{n_gt % NWS}")
                i2 = n_gt % 2
                bA, bAb = banks[2 + i2]
                bB, bBb = banks[4 + i2]
                bC, bCb = banks[6 + i2]
                for k in range(KD):
                    mm(bA, wsl[:, k, :], xn[:, k, :], k == 0, k == KD - 1, [wsl_b, xn_b], [bAb])
                tl_j = [(ti, kk) for ti, (jj, kk) in enumerate(GT) if jj == j]
                for n, (ti, kk) in enumerate(tl_j):
                    mm(bB, wrg[:, ti, :], xr[:, kk, :], n == 0, n == len(tl_j) - 1, [wrg_b, xrb[kk]], [bBb])
                for n, (ti, kk) in enumerate(tl_j):
                    mm(bC, wig[:, ti, :], xr[:, kk, :], n == 0, n == len(tl_j) - 1, [wig_b, xrb[kk]], [bCb])
                sg = gslot(t, j)
                act(gg_s[sg][0], bA, AF.Gelu_apprx_tanh, [bAb], [gg_s[sg][1]])
                act(ra_s[sg][0], bB, AF.Tanh, [bBb, rder_b], [ra_s[sg][1]], bias=rder[:, 1, j:j + 1], scale=0.5)
                act(tu_s[sg][0], bC, AF.Tanh, [bCb, rder_b], [tu_s[sg][1]], bias=rder[:, 2, j:j + 1], scale=0.5)
            for j in grp:
                tu, tu_b = tu_s[gslot(t, j)]
                stt(tu, tu, 1.0, xr[:, j, :], ALU.add, ALU.mult, [tu_b, xrb[j]], [tu_b])

        def step_c2(t, grp):
            xr, xrb = xrs[t % 2]
            for j in grp:
                sl = j % NS
                ra, ra_b = ra_s[gslot(t, j)]
                act(ra, ra, AF.Exp, [ra_b, rder_b], [ra_b], bias=rder[:, 0, j:j + 1], scale=rder[:, 0, j:j + 1])
                tt(s2_s[sl][0], ra, ra, ALU.mult, [ra_b], [s2_s[sl][1]], eng="pool")
            for j in grp:
                sl = j % NS
                s2, s2_b = s2_s[sl]
                act(s2, s2, AF.Sqrt, [s2_b, cst_b], [s2_b], bias=CQ, scale=-0.25)
            for j in grp:
                sl = j % NS
                tu, tu_b = tu_s[gslot(t, j)]
                s2, s2_b = s2_s[sl]
                tt(tu, tu, s2, ALU.mult, [tu_b, s2_b], [tu_b], eng="pool")
            for j in grp:
                ra, ra_b = ra_s[gslot(t, j)]
                tu, tu_b = tu_s[gslot(t, j)]
                yy, yy_b = y_s[j % 2]
                if t == 0:
                    P.op("dve", lambda e, yy=yy, ra=ra, tu=tu: e.tensor_tensor_scan(
                        out=yy, data0=ra, data1=tu, initial=0.0, op0=ALU.mult, op1=ALU.add),
                        reads=[ra_b, tu_b], writes=[yy_b])
                else:
                    P.op("dve", lambda e, yy=yy, ra=ra, tu=tu, j=j: e.tensor_tensor_scan(
                        out=yy, data0=ra, data1=tu, initial=state[:, j:j + 1], op0=ALU.mult, op1=ALU.add),
                        reads=[ra_b, tu_b, stb[j]], writes=[yy_b])
                gg, gg_b = gg_s[gslot(t, j)]
                cp("pool", state[:, j:j + 1], yy[:, T - 1:T], [yy_b], [stb[j]])
                tt(ob[:, j, :], gg, yy, ALU.mult, [gg_b, yy_b], [obb[j]], eng="pool")

        def step_d(t):
            for m in range(KD):
                n_out = cnt["out"]
                cnt["out"] += 1
                wsl, wsl_b = wout_s[n_out % 2]
                wload("w_out", wb["w_out"][m].rearrange("p (a b) -> p a b", a=NJ), wsl, wsl_b, f"wout{n_out % 2}")
                bk, bkb = banks[n_out % 2]
                for k in range(NJ):
                    mm(bk, wsl[:, k, :], ob[:, k, :], k == 0, k == NJ - 1, [wsl_b, obb[k]], [bkb])
                tt(h[:, m, tsl(t)], bk, h[:, m, tsl(t)], ALU.add, [bkb, hb[m][t]], [hb[m][t]])

        norm(0)
        for j in range(NJ):
            step_b(0, j)
        for t in range(NT):
            if t + 1 < NT:
                norm(t + 1)
            for j0 in range(0, NJ, NS):
                grp = list(range(j0, min(j0 + NS, NJ)))
                step_c(t, grp)
                drip(2)
                if t + 1 < NT:
                    for j in grp:
                        step_b(t + 1, j)
                step_c2(t, grp)
            step_d(t)

    def phase_F(s, l):
        A.reset(m0)
        A.limit = None
        gi = 1 if l == 0 else 4
        wn_up, wn_dn = f"w_up{l}", f"w_down{l}"
        wd, wd_b = A.alloc("wd", [NPAIR, D], BF16)
        wdv = wb[wn_dn].rearrange("p (a b) -> p a b", a=NPAIR)
        for q in range(4):
            dma("sp", wd[:, q * 6:(q + 1) * 6, :], wdv[:, q * 6:(q + 1) * 6, :], [wdram_b[wn_dn]], [wd_b], "wd")
        wup_s = [A.alloc(f"wup{i}", [KD, 2, 128], BF16) for i in range(3)]
        mk_xn = A.top
        xn, xn_b = A.alloc("xn", [KD, T], BF16)
        sq, sq_b = A.alloc("sq", [KD, T], BF16)
        rstd, rstd_b = A.alloc("rstd", [T], F32)
        actbs = [alloc_parts(A, f"actb{i}", NPAIR, [T], BF16) for i in range(2)]
        ag_s = [A.alloc(f"ag{i}", [T], F32) for i in range(2)]
        av_s = [A.alloc(f"av{i}", [T], F32) for i in range(2)]
        hal, hal_b = A.alloc("hal", [48, 2], F32)
        hal2, _ = A.alloc("hal2", [48], F32, track=False)
        fw = lambda ch, i: ffnc[:, l, ch, i:i + 1]
        fwa = lambda i: ffnc[:, l, :, i]
        n_up = 0
        n_dn = [0]
        norm_part1(gi, 0, sq, sq_b, rstd, rstd_b, 6)
        norm_part2(gi, 0, xn, xn_b, rstd, rstd_b)

        def down_m(t, m):
            actb, actbb = actbs[t % 2]
            bk, bkb = banks[4 + n_dn[0] % 2]
            n_dn[0] += 1
            for k in range(NPAIR):
                mm(bk, wd[:, k, m * 128:(m + 1) * 128], actb[:, k, :], k == 0, k == NPAIR - 1, [wd_b, actbb[k]], [bkb])
            tt(h[:, m, tsl(t)], bk, h[:, m, tsl(t)], ALU.add, [bkb, hb[m][t]], [hb[m][t]])

        def stage_b(t, c, i2):
            actb, actbb = actbs[t % 2]
            ag, ag_b = ag_s[i2]
            av, av_b = av_s[i2]
            act(ag, ag, AF.Gelu_apprx_tanh, [ag_b], [ag_b])
            tt(actb[:, c, :], ag, av, ALU.mult, [ag_b, av_b], [actbb[c]], eng="pool")

        for t in range(NT):
            par = t % 2
            if t > 0:
                tlb = [tfb[1 - par][ch] for ch in range(48)]
                T0 = tails_f[:, 1 - par, :, 0]
                T1 = tails_f[:, 1 - par, :, 1]
                tt(hal[:, :, 1], T1, fwa(0), ALU.mult, tlb + [ffnc_b], [hal_b], eng="pool")
                tt(hal[:, :, 0], T1, fwa(1), ALU.mult, tlb + [ffnc_b], [hal_b], eng="pool")
                tt(hal2, T0, fwa(0), ALU.mult, tlb + [ffnc_b], [hal_b], eng="pool")
                tt(hal[:, :, 0], hal[:, :, 0], hal2, ALU.add, [hal_b], [hal_b], eng="pool")
            prev = None
            for c in range(NPAIR):
                wsl, wsl_b = wup_s[n_up % 3]
                wload(wn_up, wb[wn_up][c].rearrange("p (a g b) -> p a g b", a=KD, g=2), wsl, wsl_b, f"wup{n_up % 3}")
                i2 = n_up % 2
                n_up += 1
                for g in range(2):
                    ch = g * NPAIR + c
                    bk, bkb = banks[2 * g + i2]
                    acc, acc_b = (ag_s if g == 0 else av_s)[i2]
                    for k in range(KD):
                        mm(bk, wsl[:, k, g, :], xn[:, k, :], k == 0, k == KD - 1, [wsl_b, xn_b], [bkb])
                    act(acc, bk, AF.Identity, [bkb, ffnc_b], [acc_b], bias=fw(ch, 3), scale=fw(ch, 2))
                    if t > 0:
                        tt(acc[:, 0:2], acc[:, 0:2], hal[:, ch, :], ALU.add, [acc_b, hal_b], [acc_b], eng="pool")
                    cp("act", tails_f[:, par, ch, :], bk[:, T - 2:T], [bkb], [tfb[par][ch]])
                for (sh, wi) in ((1, 1), (2, 0)):
                    for g in range(2):
                        ch = g * NPAIR + c
                        bk, bkb = banks[2 * g + i2]
                        acc, acc_b = (ag_s if g == 0 else av_s)[i2]
                        stt(acc[:, sh:T], bk[:, 0:T - sh], fw(ch, wi), acc[:, sh:T], ALU.mult, ALU.add,
                            [bkb, acc_b, ffnc_b], [acc_b])
                if prev is not None:
                    stage_b(t, *prev)
                prev = (c, i2)
                if c % 4 == 3:
                    drip(1)
                if t > 0 and c % 3 == 2:
                    down_m(t - 1, c // 3)
                if c == NPAIR - 6 and t + 1 < NT:
                    norm_part1(gi, t + 1, sq, sq_b, rstd, rstd_b, 6)
            stage_b(t, *prev)
            if t + 1 < NT:
                norm_part2(gi, t + 1, xn, xn_b, rstd, rstd_b)
        if l == 1 and early_out:
            save = A.top
            A.top = mk_xn
            yoF = [alloc_parts(A, f"yoF{i}", 2, [512], F32) for i in range(4)]
            A.top = save
            oblk = list(range(0, 4 * (NT - 1)))
            for m in range(KD):
                down_m(NT - 1, m)
                for _ in range(2):
                    if oblk:
                        o_block(s, oblk.pop(0), yoF, "yoF")
            while oblk:
                o_block(s, oblk.pop(0), yoF, "yoF")
        else:
            for m in range(KD):
                down_m(NT - 1, m)

    def rope_alloc(share=None):
        if share is None:
            posi, posi_b = A.alloc("posi", [T], I32)
            ang, ang_b = A.alloc("ang", [T], F32)
            zz, zz_b = A.alloc("zz", [T], F32)
            zi, zi_b = posi, posi_b
            zf, zf_b = A.alloc("zf", [T], F32)
        else:
            (posi, posi_b, ang, ang_b, zz, zz_b, zi, zi_b, zf, zf_b) = share[4:]
        Ct, Ct_b = A.alloc("Ct", [T], BF16)
        St, St_b = A.alloc("St", [T], F32)
        return (Ct, Ct_b, St, St_b, posi, posi_b, ang, ang_b, zz, zz_b, zi, zi_b, zf, zf_b)

    def rope_tables(s, t, reuse=None):
        if reuse is None:
            posi, posi_b = A.alloc("posi", [T], I32)
            ang, ang_b = A.alloc("ang", [T], F32)
            zz, zz_b = A.alloc("zz", [T], F32)
            zi, zi_b = A.alloc("zi", [T], I32)
            zf, zf_b = A.alloc("zf", [T], F32)
            Ct, Ct_b = A.alloc("Ct", [T], BF16)
            St, St_b = A.alloc("St", [T], F32)
        else:
            (Ct, Ct_b, St, St_b, posi, posi_b, ang, ang_b, zz, zz_b, zi, zi_b, zf, zf_b) = reuse
        dma("sp", posi, pos_d[s:s + 1, tsl(t)].broadcast_to([128, T]), [], [posi_b], f"posi{t % 2}")
        cp("dve", ang, posi, [posi_b], [ang_b])
        ts(ang, ang, invf[:, 0:1], None, ALU.mult, ALU.bypass, [ang_b, invf_b], [ang_b])
        for (dst, dst_b, off) in ((St, St_b, 0.5), (Ct, Ct_b, 0.75)):
            ts(zz, ang, 1.0 / TWO_PI, off, ALU.mult, ALU.add, [ang_b], [zz_b])
            cp("dve", zi, zz, [zz_b], [zi_b])
            cp("dve", zf, zi, [zi_b], [zf_b])
            tt(zz, zz, zf, ALU.subtract, [zz_b, zf_b], [zz_b])
            stt(zz, zz, 0.0, zz, ALU.is_lt, ALU.add, [zz_b], [zz_b])
            act(dst, zz, AF.Sin, [zz_b, cst_b], [dst_b], bias=CNPI, scale=TWO_PI)
        return (Ct, Ct_b, St, St_b, posi, posi_b, ang, ang_b, zz, zz_b, zi, zi_b, zf, zf_b)

    def hnr_tmps(n):
        return [dict(qsq=A.alloc("qsq", [T], BF16), rq=A.alloc("rq", [T], F32), qnb=A.alloc("qnb", [T], BF16),
                     t1=A.alloc("t1", [T], BF16), t2=A.alloc("t2", [T], BF16)) for _ in range(n)]

    def hnr_s1(src, tm):
        (src_bk, src_bkb) = src
        qsq, qsq_b = tm["qsq"]
        act(qsq, src_bk, AF.Square, [src_bkb], [qsq_b])

    def hnr_s2(src, gcol, tm, bank2):
        (src_bk, src_bkb) = src
        qsq, qsq_b = tm["qsq"]
        rq, rq_b = tm["rq"]
        qnb, qnb_b = tm["qnb"]
        b2, b2b = bank2
        mm(b2, bones_bf, qsq, True, True, [cmat_b, qsq_b], [b2b])
        act(rq, b2, AF.Ln, [b2b, cst_b], [rq_b], bias=CE, scale=1.0 / HD)
        act(rq, rq, AF.Exp, [rq_b], [rq_b], scale=-0.5)
        stt(qnb, src_bk, gcol, rq, ALU.mult, ALU.mult, [src_bkb, qkg_b, rq_b], [qnb_b])

    def hnr_s3(tm, bank3, Ct, Ct_b, St, St_b, dst, dst_bufs):
        qnb, qnb_b = tm["qnb"]
        t1, t1_b = tm["t1"]
        t2, t2_b = tm["t2"]
        b3, b3b = bank3
        mm(b3, rt_bf, qnb, True, True, [cmat_b, qnb_b], [b3b])
        tt(t1, qnb, Ct, ALU.mult, [qnb_b, Ct_b], [t1_b])
        tt(t2, b3, St, ALU.mult, [b3b, St_b], [t2_b])
        tt(dst, t1, t2, ALU.add, [t1_b, t2_b], dst_bufs)

    kvstate = {}

    def alloc_kv():
        save = A.top
        A.top = kv_lo
        A.in_kv = True
        Kf, Kfb = alloc_parts(A, "Kf", NT, [T], BF16)
        Kf = Kf.rearrange("p a b -> p (a b)")
        Vsb, Vb = alloc_parts(A, "Vsb", NT, [4, 128], BF16)
        Vsb = Vsb.rearrange("p a b c -> p (a b) c")
        esr, esr_b = A.alloc("esr", [NH, 128], BF16)
        A.in_kv = False
        A.top = save
        return Kf, Kfb, Vsb, Vb, esr, esr_b

    def phase_KA(s):
        A.reset(m0)
        A.limit = kv_lo
        Kf, Kfb, Vsb, Vb, esr, esr_b = alloc_kv()
        cp("dve", esr[0:1, :, :], sk[0:1, :].unsqueeze(2).broadcast_to([1, NH, 128]), [sk_b], [esr_b])
        wk, wk_b = A.alloc("wk", [KD, 128], BF16)
        wv, wv_b = A.alloc("wv", [KD, 128], BF16)
        wload("w_k", wb["w_k"].rearrange("p (a b) -> p a b", a=KD), wk, wk_b, "wk")
        wload("w_v", wb["w_v"].rearrange("p (a b) -> p a b", a=KD), wv, wv_b, "wv")
        wq, wq_b = A.alloc("wq", [KD, D], BF16)
        wqv = wb["w_q"].rearrange("p (a b) -> p a b", a=KD)
        for q in range(2):
            dma("sp", wq[:, q * 4:(q + 1) * 4, :], wqv[:, q * 4:(q + 1) * 4, :], [wdram_b["w_q"]], [wq_b], "wq")
        wo, wo_b = A.alloc("wo", [KD, D], BF16)
        wov = wb["w_o"].rearrange("p (a b) -> p a b", a=NH)
        for hf in range(2):
            dma("sp", wo[hf * 64:(hf + 1) * 64, :, :], wov[:, hf:NH:2, :], [wdram_b["w_o"]], [wo_b], "wo")
        sqk, sqk_b = A.alloc("sqk", [KD, T], BF16)
        xnk, xnk_b = A.alloc("xnk", [KD, T], BF16)
        rstdk, rstdk_b = A.alloc("rstdk", [T], F32)
        xn, xn_b = A.alloc("xn", [KD, T], BF16)
        rstd, rstd_b = rstdk, rstdk_b
        qrot, qrb = alloc_parts(A, "qrot", KD, [T], BF16)
        attT, atb = alloc_parts(A, "attT", KD, [T], BF16)
        p_s = [A.alloc(f"pT{i}", [T], BF16) for i in range(3)]
        rec_s = [A.alloc(f"rec{i}", [256], F32) for i in range(2)]
        tms = hnr_tmps(4)
        tmk = hnr_tmps(1)[0]
        ropes = [rope_alloc()]
        ropes.append(rope_alloc(share=ropes[0]))
        n_o = 0
        n_w = 0
        BK, BV = banks[6], banks[7]

        def k_thunks(t):
            th = []
            th.append(lambda: rope_tables(s, t, reuse=ropes[t % 2]))
            th.append(lambda: act(sqk, h[:, :, tsl(t)], AF.Square, [hb[k][t] for k in range(KD)], [sqk_b]))

            def n1():
                for k in range(KD):
                    mm(BV[0], ones_bf, sqk[:, k, :], k == 0, k == KD - 1, [sqk_b, cmat_b], [BV[1]])
                act(rstdk, BV[0], AF.Ln, [BV[1], cst_b], [rstdk_b], bias=CE, scale=1.0 / D)
                act(rstdk, rstdk, AF.Exp, [rstdk_b], [rstdk_b], scale=-0.5)
            th.append(n1)
            for k0 in range(0, KD, 2):
                def n2(k0=k0):
                    for k in (k0, k0 + 1):
                        stt(xnk[:, k, :], h[:, k, tsl(t)], gains[:, 2, k:k + 1], rstdk, ALU.mult, ALU.mult,
                            [hb[k][t], gains_b, rstdk_b], [xnk_b])
                th.append(n2)

            def kp():
                for k in range(KD):
                    mm(BK[0], wk[:, k, :], xnk[:, k, :], k == 0, k == KD - 1, [wk_b, xnk_b], [BK[1]])
                hnr_s1(BK, tmk)
            th.append(kp)

            def vp():
                for blk in range(4):
                    for k in range(KD):
                        mm(BV[0][:, blk * 128:(blk + 1) * 128], xnk[:, k, blk * 128:(blk + 1) * 128], wv[:, k, :],
                           k == 0, k == KD - 1, [wv_b, xnk_b], [BV[1]])
                cp("act", Vsb[:, t * 4:(t + 1) * 4, :], BV[0].rearrange("p (a b) -> p a b", a=4), [BV[1]], [Vb[t]])
            th.append(vp)
            th.append(lambda: hnr_s2(BK, qkg[:, 1:2], tmk, BV))
            th.append(lambda: hnr_s3(tmk, BV, ropes[t % 2][0], ropes[t % 2][1], ropes[t % 2][2], ropes[t % 2][3],
                                     Kf[:, tsl(t)], [Kfb[t]]))
            return th

        for f in k_thunks(0):
            f()
        for t in range(NT):
            norm_part1(3, t, sqk, sqk_b, rstd, rstd_b, 6)
            norm_part2(3, t, xn, xn_b, rstd, rstd_b)
            Ct, Ct_b, St, St_b = ropes[t % 2][:4]
            kth = k_thunks(t + 1) if t + 1 < NT else []
            if SEQ_K:
                while kth:
                    kth.pop(0)()
            for g4 in range(4):
                bo = 4 * (g4 % 2)
                tmq = [tms[(2 * g4 + u) % 4] for u in range(2)]
                cs = [2 * g4, 2 * g4 + 1]
                for u, c in enumerate(cs):
                    bQ = banks[bo + u]
                    for k in range(KD):
                        mm(bQ[0], wq[:, k, c * 128:(c + 1) * 128], xn[:, k, :], k == 0, k == KD - 1, [wq_b, xn_b], [bQ[1]])
                for u, c in enumerate(cs):
                    hnr_s1(banks[bo + u], tmq[u])
                for u, c in enumerate(cs):
                    (src_bk, src_bkb) = banks[bo + u]
                    qsq, qsq_b = tmq[u]["qsq"]
                    b2, b2b = banks[bo + 2 + u]
                    mm(b2, bones_bf, qsq, True, True, [cmat_b, qsq_b], [b2b])
                for u, c in enumerate(cs):
                    rq, rq_b = tmq[u]["rq"]
                    b2, b2b = banks[bo + 2 + u]
                    act(rq, b2, AF.Ln, [b2b, cst_b], [rq_b], bias=CE, scale=1.0 / HD)
                for u, c in enumerate(cs):
                    rq, rq_b = tmq[u]["rq"]
                    act(rq, rq, AF.Exp, [rq_b], [rq_b], scale=-0.5)
                for u, c in enumerate(cs):
                    (src_bk, src_bkb) = banks[bo + u]
                    rq, rq_b = tmq[u]["rq"]
                    qnb, qnb_b = tmq[u]["qnb"]
                    stt(qnb, src_bk, qkg[:, 0:1], rq, ALU.mult, ALU.mult, [src_bkb, qkg_b, rq_b], [qnb_b])
                for u, c in enumerate(cs):
                    qnb, qnb_b = tmq[u]["qnb"]
                    b3, b3b = banks[bo + 2 + u]
                    mm(b3, rt_bf, qnb, True, True, [cmat_b, qnb_b], [b3b])
                for u, c in enumerate(cs):
                    qnb, qnb_b = tmq[u]["qnb"]
                    t1, t1_b = tmq[u]["t1"]
                    tt(t1, qnb, Ct, ALU.mult, [qnb_b, Ct_b], [t1_b])
                for u, c in enumerate(cs):
                    t2, t2_b = tmq[u]["t2"]
                    b3, b3b = banks[bo + 2 + u]
                    tt(t2, b3, St, ALU.mult, [b3b, St_b], [t2_b])
                for u, c in enumerate(cs):
                    t1, t1_b = tmq[u]["t1"]
                    t2, t2_b = tmq[u]["t2"]
                    tt(qrot[:, c, :], t1, t2, ALU.add, [t1_b, t2_b], [qrb[c]])
            jobs = []
            for i in range(4):
                nb = t * 4 + i
                kbs = [kb for kb in (nb - 1, nb) if kb >= 0]
                for g in range(2):
                    for hg in range(2):
                        for n, kb in enumerate(kbs):
                            jobs.append((i, nb, g, hg, n, kb, len(kbs)))

            def emit_S(jn):
                i, nb, g, hg, n, kb, nk = jobs[jn]
                ps = slice(g * 64, (g + 1) * 64)
                bS, bSb = banks[jn % 3]
                mm(bS, Kf[ps, kb * 128:(kb + 1) * 128], qrot[ps, 4 * hg:4 * hg + 4, i * 128:(i + 1) * 128],
                   True, True, [Kfb[kb // 4]] + [qrb[4 * hg + u] for u in range(4)], [bSb])

            LA = 2
            pending = []
            for jn in range(min(LA, len(jobs))):
                emit_S(jn)
            for jn, (i, nb, g, hg, n, kb, nk) in enumerate(jobs):
                ps = slice(g * 64, (g + 1) * 64)
                bS, bSb = banks[jn % 3]
                pT, pT_b = p_s[jn % 3]
                act(pT, bS, AF.Exp, [bSb], [pT_b], scale=float(HD) ** -0.5)
                pT3 = pT.rearrange("p (a b) -> p a b", a=4)
                if kb == nb:
                    P.op("pool", lambda e, pT3=pT3: e.affine_select(
                        out=pT3, in_=pT3, pattern=[[0, 4], [1, 128]], compare_op=ALU.is_ge, fill=0.0,
                        base=0, channel_multiplier=-1), reads=[pT_b], writes=[pT_b])
                else:
                    P.op("pool", lambda e, pT3=pT3: e.affine_select(
                        out=pT3, in_=pT3, pattern=[[0, 4], [-1, 128]], compare_op=ALU.is_gt, fill=0.0,
                        base=0, channel_multiplier=1), reads=[pT_b], writes=[pT_b])
                if jn + LA < len(jobs):
                    emit_S(jn + LA)
                while pending and pending[0][0] <= jn:
                    pending.pop(0)[1]()
                if kth and jn % 2 == 1:
                    kth.pop(0)()
                if n == 0:
                    n_o += 1
                bOD, bODb = banks[3 + n_o % 3]
                h0 = 8 * g + 4 * hg
                for par_ in range(2):
                    po = slice(par_ * 64, (par_ + 1) * 64)
                    P.op("pe", lambda e, o=bOD[po, 0:256], l=Vsb[:, kb, ps], r=pT3[:, par_:4:2, :], st=(n == 0):
                         e.matmul(o, l, r, start=st, stop=False, skip_group_check=True),
                         reads=[Vb[kb // 4], pT_b], writes=[bODb])
                    P.op("pe", lambda e, o=bOD[po, 256:512], l=ones_bf[:, 0:64], r=pT3[:, par_:4:2, :]:
                         e.matmul(o, l, r, start=False, stop=False, skip_group_check=True),
                         reads=[cmat_b, pT_b], writes=[bODb])
                if n == nk - 1:
                    for par_ in range(2):
                        po = slice(par_ * 64, (par_ + 1) * 64)
                        P.op("pe", lambda e, o=bOD[po, 256:512], l=ones_bf[0:1, 0:64], r=esr[0:1, h0 + par_:h0 + 4:2, :]:
                             e.matmul(o, l, r, start=False, stop=True, skip_group_check=True),
                             reads=[cmat_b, esr_b], writes=[bODb])

                    def fin(rec_t=rec_s[n_o % 2], bOD=bOD, bODb=bODb, pr0=h0 // 2, i=i):
                        rec, rec_b = rec_t
                        act(rec, bOD[:, 256:512], AF.Ln, [bODb], [rec_b])
                        act(rec, rec, AF.Exp, [rec_b], [rec_b], scale=-1.0)
                        tt(attT[:, pr0:pr0 + 2, i * 128:(i + 1) * 128],
                           bOD[:, 0:256].rearrange("p (a b) -> p a b", a=2), rec.rearrange("p (a b) -> p a b", a=2),
                           ALU.mult, [bODb, rec_b], [atb[pr0], atb[pr0 + 1]])
                    pending.append((jn + nk, fin))
            while pending:
                pending.pop(0)[1]()
            while kth:
                kth.pop(0)()
            for m in range(KD):
                bk, bkb = banks[1 if n_w % 2 else 0]
                n_w += 1
                for c in range(KD):
                    mm(bk, wo[:, c, m * 128:(m + 1) * 128], attT[:, c, :], c == 0, c == KD - 1, [wo_b, atb[c]], [bkb])
                tt(h[:, m, tsl(t)], bk, h[:, m, tsl(t)], ALU.add, [bkb, hb[m][t]], [hb[m][t]])


    def phase_X(s):
        A.reset(m0)
        xin = [A.alloc(f"xin{i}", [D], F32) for i in range(4)]
        for blk in range(S // 128):
            xt, xb_ = xin[blk % 4]
            dma("sp", xt, x_d[s, blk * 128:(blk + 1) * 128, :], [], [xb_], f"xin{blk % 4}")
            t = blk // 4
            for half in range(2):
                bk, bkb = banks[(blk * 2 + half) % 8]
                for q in range(4):
                    k = half * 4 + q
                    P.op("pe", lambda e, o=bk[:, q * 128:(q + 1) * 128], i=xt[:, k * 128:(k + 1) * 128]:
                         e.transpose(out=o, in_=i, identity=identf), reads=[xb_, identf_b], writes=[bkb])
                eng = "act" if half == 0 else "dve"
                cp(eng, h[:, half * 4:half * 4 + 4, blk * 128:(blk + 1) * 128],
                   bk.rearrange("p (a b) -> p a b", a=4), [bkb], [hb[k][t] for k in range(half * 4, half * 4 + 4)])

    def o_block(s, blk, yo, semp):
        yt, ybs = yo[blk % 4]
        t = blk // 4
        for half in range(2):
            bk, bkb = banks[(blk * 2 + half) % 4]
            for q in range(4):
                k = half * 4 + q
                P.op("pe", lambda e, o=bk[:, q * 128:(q + 1) * 128], i=h[:, k, blk * 128:(blk + 1) * 128]:
                     e.transpose(out=o, in_=i, identity=identf), reads=[hb[k][t], identf_b], writes=[bkb])
            eng = "act" if half == 0 else "dve"
            cp(eng, yt[:, half, :], bk, [bkb], [ybs[half]])
        dma("sp", y_d[s, blk * 128:(blk + 1) * 128, :], yt.rearrange("p a b -> p (a b)"), list(ybs), [], f"{semp}{blk % 4}")

    def phase_O(s, blocks):
        A.reset(m0)
        yo = [alloc_parts(A, f"yo{i}", 2, [512], F32) for i in range(4)]
        for blk in blocks:
            o_block(s, blk, yo, "yo")

    for s in range(nseq):
        phase_X(s)
        if s == 0:
            for n in ("w_up0", "w_down0", "w_k", "w_v", "w_q", "w_o", "w_up1", "w_down1"):
                cast_weight(n, defer=True)
        if last_stage >= 1:
            phase_R(s)
        flush_casts(("w_up0", "w_down0"))
        if last_stage >= 2:
            phase_F(s, 0)
        flush_casts(("w_k", "w_v", "w_q", "w_o", "w_up1", "w_down1"))
        if last_stage >= 4:
            phase_KA(s)
        if last_stage >= 5:
            phase_F(s, 1)
            phase_O(s, range(4 * (NT - 1), S // 128))
        else:
            phase_O(s, range(S // 128))

    final_ops = [o for o in P.q["sp"] if o.is_dma]
    engines = {"pe": nc.tensor, "act": nc.scalar, "dve": nc.vector, "pool": nc.gpsimd, "sp": nc.sync}
    P.op("sp", lambda e: e.nop(), extra_deps=final_ops)
    P.op("pool", lambda e: e.nop(), extra_deps=[o for o in P.q["pool"] if o.is_dma])

    with nc.Block() as block:
        def run(ename):
            def body(eng):
                Pn = P
                waited = {}
                for o in Pn.q[ename]:
                    need = {}
                    for d in o.deps:
                        kk = id(d.sem)
                        if kk not in need or need[kk][1] < d.val:
                            need[kk] = (d.sem, d.val)
                    for kk, (sem, val) in need.items():
                        if waited.get(kk, 0) < val:
                            eng.wait_ge(sem, val)
                            waited[kk] = val
                    inst = o.fn(eng)
                    if o.signal and inst is not None:
                        inst.then_inc(o.sem, 16 if o.is_dma else 1)
            return body

        SEG = 12000
        for e in ENGS:
            cnt = 0
            si = 0
            for o in P.q[e]:
                if o.is_dma:
                    c = P.dma_counts.get(id(o.dsem), 0) + 16
                    P.dma_counts[id(o.dsem)] = c
                    o.sem, o.val = o.dsem, c
                elif o.signal:
                    if cnt >= SEG:
                        si += 1
                        cnt = 0
                    cnt += 1
                    o.sem, o.val = sems[e][si], cnt
        block.tensor(run("pe"))
        block.scalar(run("act"))
        block.vector(run("dve"))
        block.gpsimd(run("pool"))
        block.sync(run("sp"))
    es.close()
    return nc


def _img(w, kc):
    K, N = w.shape
    assert K == kc * 128
    return np.ascontiguousarray(w.reshape(kc, 128, N).transpose(1, 0, 2).reshape(128, kc * N))


def prepare_shared(inp):
    f = np.float32
    o = {}
    pad = LWP - LW
    w_in = np.pad(np.asarray(inp["a_w_in"][0], f), ((0, 0), (0, pad)))
    w_gate = np.pad(np.asarray(inp["a_w_gate"][0], f), ((0, 0), (0, pad)))

    def chunked_cols(w):
        return np.ascontiguousarray(w.reshape(KD, 128, NJ, 128).transpose(2, 1, 0, 3).reshape(NJ, 128, KD * 128))
    o["w_in"] = chunked_cols(w_in)
    o["w_gate"] = chunked_cols(w_gate)
    for nm, key in (("w_rg", "a_w_rg"), ("w_ig", "a_w_ig")):
        wbd = np.zeros((LWP, LWP), f)
        blk = np.asarray(inp[key][0], f)
        for b in range(8):
            wbd[b * LBLK:(b + 1) * LBLK, b * LBLK:(b + 1) * LBLK] = blk[b]
        tiles = np.stack([wbd[kk * 128:(kk + 1) * 128, j * 128:(j + 1) * 128] for (j, kk) in GT], 0)
        o[nm] = np.ascontiguousarray(tiles.transpose(1, 0, 2).reshape(128, NGT * 128))
    w_out = np.pad(np.asarray(inp["a_w_out"][0], f), ((0, pad), (0, 0)))
    o["w_out"] = np.ascontiguousarray(w_out.reshape(NJ, 128, KD, 128).transpose(2, 1, 0, 3).reshape(KD, 128, NJ * 128))
    for l in range(2):
        wu = np.asarray(inp["f_w_up"][l], f)
        o[f"w_up{l}"] = np.ascontiguousarray(
            wu.reshape(KD, 128, 2, NPAIR, 128).transpose(3, 1, 0, 2, 4).reshape(NPAIR, 128, KD * 2 * 128))
        o[f"w_down{l}"] = _img(np.asarray(inp["f_w_down"][l], f), NPAIR)
    o["w_k"] = _img(np.asarray(inp["w_k"], f), KD)
    o["w_v"] = _img(np.asarray(inp["w_v"], f), KD)
    wq = np.asarray(inp["b_w_q"][0], f)
    perm = np.concatenate([np.concatenate([np.arange(c * 64, c * 64 + 64), np.arange((8 + c) * 64, (8 + c) * 64 + 64)])
                           for c in range(8)])
    o["w_q"] = _img(np.ascontiguousarray(wq[:, perm]), KD)
    wo = np.asarray(inp["b_w_o"][0], f)
    o["w_o"] = np.ascontiguousarray(wo.reshape(NH, 64, D).transpose(1, 0, 2).reshape(64, NH * D))

    def vimg(v, kc):
        return np.asarray(v, f).reshape(kc, 128).T
    gains = np.stack([vimg(inp["a_norm"][0], KD), vimg(inp["f_norm"][0], KD), vimg(inp["kv_norm"], KD),
                      vimg(inp["b_norm"][0], KD), vimg(inp["f_norm"][1], KD)], 1)
    o["gains"] = np.ascontiguousarray(gains.reshape(128, 5 * KD))

    def padv(v):
        return np.pad(np.asarray(v, f), (0, pad))
    cw = np.asarray(inp["a_conv_w"][0], f)
    rec = np.stack([vimg(padv(cw[0]), NJ), vimg(padv(cw[1]), NJ), vimg(padv(cw[2]), NJ), vimg(padv(cw[3]), NJ),
                    vimg(padv(inp["a_conv_b"][0]), NJ), vimg(padv(inp["a_b_rg"][0]), NJ),
                    vimg(padv(inp["a_b_ig"][0]), NJ), vimg(padv(inp["a_lam"][0]), NJ)], 2)
    o["rec_c"] = np.ascontiguousarray(rec.reshape(128, NJ * 8))
    ffn = np.zeros((128, 2, 48, 4), f)
    for l in range(2):
        fw = np.asarray(inp["f_conv_w"][l], f)
        for i in range(3):
            ffn[:, l, :, i] = vimg(fw[i], 48)
        ffn[:, l, :, 3] = vimg(inp["f_conv_b"][l], 48)
    o["ffn_c"] = np.ascontiguousarray(ffn.reshape(128, -1))
    qn = np.asarray(inp["q_norm"][0], f)
    kn = np.asarray(inp["k_norm"], f)
    o["qk_g"] = np.ascontiguousarray(np.stack([np.tile(qn, 2), np.tile(kn, 2)], 1))
    o["sinks_row"] = np.asarray(inp["sinks"][0], f).reshape(1, NH)
    o["ident_f"] = np.eye(128, dtype=f)
    ones = np.ones((128, 128), f)
    bones = np.zeros((128, 128), f)
    bones[:64, :64] = 1
    bones[64:, 64:] = 1
    R = np.zeros((128, 128), f)
    for hh in range(2):
        for d in range(8):
            R[hh * 64 + d, hh * 64 + d + 8] = -1
            R[hh * 64 + d + 8, hh * 64 + d] = 1
    o["cmat"] = np.ascontiguousarray(np.concatenate([ones, bones, R.T], 1))
    inv_freq = (ROPE_THETA ** (-np.arange(0, ROPE_DIM, 2, dtype=np.float32) / ROPE_DIM)).astype(f)
    iv = np.zeros((128, 1), f)
    for p in range(128):
        d = p % 64
        if d < ROPE_DIM:
            iv[p, 0] = inv_freq[d % 8]
    o["invf"] = iv
    return o


_CACHE = {}


def kernel(**inputs):
    stop_after = inputs.pop("_stop_after", "F1")
    ncores = inputs.pop("_ncores", NCORES)
    x = np.asarray(inputs["x"], np.float32)
    pos = np.asarray(inputs["positions"], np.int32)
    shared = prepare_shared(inputs)
    key = stop_after
    if key not in _CACHE:
        _CACHE[key] = build_program(stop_after)
    nc = _CACHE[key]
    in_maps = []
    for c in range(ncores):
        m = dict(shared)
        m["x"] = np.ascontiguousarray(x[c * NSEQ:(c + 1) * NSEQ])
        m["pos"] = np.ascontiguousarray(pos[c * NSEQ:(c + 1) * NSEQ])
        in_maps.append(m)
    res = run_bass_kernel_spmd(nc, in_maps, core_ids=list(range(ncores)))
    out = np.concatenate([np.asarray(r["y"]) for r in res.results], axis=0)
    return out.astype(np.float32)
```
